# Optimizing a Trainium2 kernel written in Bass

```python
import jax, jax.numpy as jnp
from jax import lax
import numpy as np

D_MODEL = 1024
BATCH = 16
SEQ = 4096
DEPTH = 2
DEC_BATCH = 8
DEC_SEQ = 64
PAST_LEN = 4096

CHUNK = 64
MLP_CHUNK = 128
D_MIX = D_MODEL
D_GMLP = D_MIX // 2
D_POOL = D_MIX - D_GMLP
GMLP_HEADS = 8
GMLP_HEAD_DIM = D_GMLP // GMLP_HEADS
POOL_WINDOWS = (2, 4, 8, 16)
POOL_GROUPS = len(POOL_WINDOWS)
POOL_GROUP_DIM = D_POOL // POOL_GROUPS
POOL_HIST = max(POOL_WINDOWS) - 1
D_IN_PROJ = 2 * D_GMLP + D_POOL
D_FF = ((8 * D_MODEL + 3 * 256 - 1) // (3 * 256)) * 256
EPS = 1e-6

kernel_name = "hybrid_gmlp_pool_streaming_encoder_step"


def rmsnorm(x, g):
    x32 = x.astype(jnp.float32)
    y = x32 * lax.rsqrt(jnp.mean(x32 * x32, axis=-1, keepdims=True) + EPS)
    return (y * g.astype(jnp.float32)).astype(x.dtype)


def gmlp_spatial(v, w_s, b_s):
    B, T, C = v.shape
    n_blk = -(-T // MLP_CHUNK)
    pad = n_blk * MLP_CHUNK - T
    vp = jnp.pad(v, ((0, 0), (0, pad), (0, 0))).reshape(B, n_blk, MLP_CHUNK, GMLP_HEADS, GMLP_HEAD_DIM)
    cid = jnp.arange(MLP_CHUNK) // CHUNK
    mask = cid[:, None] >= cid[None, :]
    w = jnp.where(mask[None], w_s, 0).astype(v.dtype)
    out = jnp.einsum('hij,bnjhc->bnihc', w, vp) + b_s.T.astype(v.dtype)[None, None, :, :, None]
    return out.reshape(B, n_blk * MLP_CHUNK, C)[:, :T]


def pool_mix(xb, hist, start_pos, w_pool, pool_scale):
    B, T, C = xb.shape
    xc = jnp.concatenate([hist, xb], axis=1).astype(jnp.float32)
    cs = jnp.pad(jnp.cumsum(xc, axis=1), ((0, 0), (1, 0), (0, 0)))
    pos = start_pos + jnp.arange(T)
    means = []
    for g, w in enumerate(POOL_WINDOWS):
        lo, hi = g * POOL_GROUP_DIM, (g + 1) * POOL_GROUP_DIM
        end = cs[:, POOL_HIST + 1:, lo:hi]
        beg = cs[:, POOL_HIST + 1 - w:POOL_HIST + 1 - w + T, lo:hi]
        cnt = jnp.minimum(pos + 1, w).astype(jnp.float32)
        means.append((end - beg) / cnt[None, :, None])
    mean = jnp.concatenate(means, axis=-1)
    d = (mean - xb.astype(jnp.float32)).astype(xb.dtype).reshape(B, T, POOL_GROUPS, POOL_GROUP_DIM)
    y = jnp.einsum('btgc,gcd->btgd', d, w_pool).reshape(B, T, C) * pool_scale
    new_hist = xc[:, -POOL_HIST:].astype(xb.dtype)
    return y, new_hist


def layer(x, hist, start_pos, g_mix, w_in, g_v, w_s, b_s, w_pool, pool_scale, g_branch, w_out,
          g_ffn, w_gate, w_up, w_down):
    h = rmsnorm(x, g_mix)
    z = h @ w_in
    u = jax.nn.gelu(z[..., :D_GMLP], approximate=False)
    v = rmsnorm(jax.nn.gelu(z[..., D_GMLP:2 * D_GMLP], approximate=False), g_v)
    xb = z[..., 2 * D_GMLP:]
    ya = u * gmlp_spatial(v, w_s, b_s)
    yb, new_hist = pool_mix(xb, hist, start_pos, w_pool, pool_scale)
    ym = jnp.concatenate([rmsnorm(ya, g_branch[:D_GMLP]), rmsnorm(yb, g_branch[D_GMLP:])], axis=-1)
    x = x + ym @ w_out
    h = rmsnorm(x, g_ffn)
    x = x + (jax.nn.silu(h @ w_gate) * (h @ w_up)) @ w_down
    return x, v, new_hist


def setup_inputs(seed: int = 0) -> dict:
    key = jax.random.key(seed)
    ks = jax.random.split(key, 17)
    f32 = jnp.float32

    def nrm(k, shape, scale):
        return jax.random.normal(k, shape, f32) * scale

    def gain(k, shape):
        return 1.0 + 0.05 * jax.random.normal(k, shape, f32)

    return {
        'x_prompt': nrm(ks[0], (BATCH, SEQ, D_MODEL), 1.0),
        'x_sample': nrm(ks[1], (DEC_BATCH, DEC_SEQ, D_MODEL), 1.0),
        'state_pool': nrm(ks[2], (DEPTH, DEC_BATCH, POOL_HIST, D_POOL), 1.0),
        'g_mix': gain(ks[3], (DEPTH, D_MODEL)),
        'w_in': nrm(ks[4], (DEPTH, D_MODEL, D_IN_PROJ), D_MODEL ** -0.5),
        'g_v': gain(ks[5], (DEPTH, D_GMLP)),
        'w_spatial': nrm(ks[6], (DEPTH, GMLP_HEADS, MLP_CHUNK, MLP_CHUNK), MLP_CHUNK ** -0.5),
        'b_spatial': gain(ks[7], (DEPTH, GMLP_HEADS, MLP_CHUNK)),
        'w_pool': nrm(ks[8], (DEPTH, POOL_GROUPS, POOL_GROUP_DIM, POOL_GROUP_DIM), POOL_GROUP_DIM ** -0.5),
        'pool_scale': gain(ks[9], (DEPTH, D_POOL)),
        'g_branch': gain(ks[10], (DEPTH, D_MIX)),
        'w_out': nrm(ks[11], (DEPTH, D_MIX, D_MODEL), D_MIX ** -0.5),
        'g_ffn': gain(ks[12], (DEPTH, D_MODEL)),
        'w_gate': nrm(ks[13], (DEPTH, D_MODEL, D_FF), D_MODEL ** -0.5),
        'w_up': nrm(ks[14], (DEPTH, D_MODEL, D_FF), D_MODEL ** -0.5),
        'w_down': nrm(ks[15], (DEPTH, D_FF, D_MODEL), D_FF ** -0.5),
        'g_final': gain(ks[16], (D_MODEL,)),
    }


def reference(x_prompt, x_sample, state_pool, g_mix, w_in, g_v, w_spatial, b_spatial, w_pool, pool_scale,
              g_branch, w_out, g_ffn, w_gate, w_up, w_down, g_final):
    xp, xs = x_prompt, x_sample
    hist_p_out, hist_s_out, v_s_out = [], [], []
    for l in range(DEPTH):
        params = (g_mix[l], w_in[l], g_v[l], w_spatial[l], b_spatial[l], w_pool[l], pool_scale[l],
                  g_branch[l], w_out[l], g_ffn[l], w_gate[l], w_up[l], w_down[l])
        hist0 = jnp.zeros((xp.shape[0], POOL_HIST, D_POOL), xp.dtype)
        xp, _, hp = layer(xp, hist0, 0, *params)
        xs, vs, hs = layer(xs, state_pool[l], PAST_LEN, *params)
        hist_p_out.append(hp)
        hist_s_out.append(hs)
        v_s_out.append(vs)
    y_prompt = rmsnorm(xp, g_final)
    y_sample = rmsnorm(xs, g_final)
    new_state_pool_prompt = jnp.stack(hist_p_out, axis=0)
    new_state_pool_sample = jnp.stack(hist_s_out, axis=0)
    new_gmlp_v_sample = jnp.stack(v_s_out, axis=0)
    return (y_prompt, y_sample, new_state_pool_prompt, new_state_pool_sample, new_gmlp_v_sample)
```

```python
import numpy as np
import concourse.bass as bass
import concourse.mybir as mybir
from concourse.bass_utils import run_bass_kernel_spmd

F32 = mybir.dt.float32
BF16 = mybir.dt.bfloat16
AF = mybir.ActivationFunctionType
ALU = mybir.AluOpType

EPS = 1e-6
D = 1024
DG = 512
DFF = 2816
KC = 8
FC = 22
HIST = 15
WINDOWS = (2, 4, 8, 16)
N_CORES = 8


class Buf:
    __slots__ = ("name", "w", "rs", "const", "tok")

    def __init__(self, name, tok=None):
        self.name = name
        self.w = None
        self.rs = {}
        self.const = False
        self.tok = tok


class Op:
    __slots__ = ("eng", "fn", "deps", "lane", "signal", "sig", "clock", "waits", "idx")


class Prog:
    ENGS = ("pe", "act", "dve", "pool", "sp")

    def __init__(self):
        self.ops = []
        self.lane_last = {}
        self.out_ops = []

    def op(self, eng, fn, reads=(), writes=(), lane=None, is_out=False):
        o = Op()
        o.eng = eng
        o.fn = fn
        o.lane = lane
        o.signal = lane is not None
        o.idx = len(self.ops)
        o.sig = None
        deps = {}
        reads = list(reads)
        for b in list(reads) + list(writes):
            if b.tok is not None and b.tok not in reads:
                reads.append(b.tok)

        def add(d):
            if d is None:
                return
            if d.lane is None and d.eng == "pe" and eng == "pe" and lane is None:
                return
            key = d.eng if d.lane is None else ("lane", d.lane)
            cur = deps.get(key)
            if cur is None or cur.idx < d.idx:
                deps[key] = d

        for b in reads:
            add(b.w)
        for b in writes:
            add(b.w)
            for r in b.rs.values():
                add(r)
        if lane is not None:
            add(self.lane_last.get(lane))
            self.lane_last[lane] = o
        mykey = eng if lane is None else ("lane", lane)
        for b in reads:
            if not b.const:
                b.rs[mykey] = o
        for b in writes:
            b.w = o
            b.rs = {}
        o.deps = sorted(deps.values(), key=lambda d: d.idx)
        for d in o.deps:
            d.signal = True
        self.ops.append(o)
        if is_out:
            self.out_ops.append(o)
        return o

    def finalize(self, nc):
        fin = Op()
        fin.eng = "sp"
        fin.fn = None
        fin.lane = None
        fin.signal = False
        fin.idx = len(self.ops)
        fin.sig = None
        dd = {}
        for d in self.out_ops:
            key = ("lane", d.lane)
            if key not in dd or dd[key].idx < d.idx:
                dd[key] = d
        fin.deps = sorted(dd.values(), key=lambda d: d.idx)
        self.ops.append(fin)

        eng_count = {e: 0 for e in self.ENGS}
        lane_count = {}
        known = {e: {} for e in self.ENGS}
        nwaits = 0
        for o in self.ops:
            kn = known[o.eng]
            waits = {}
            for d in o.deps:
                key, val = d.sig
                if kn.get(key, 0) >= val:
                    continue
                if waits.get(key, 0) < val:
                    waits[key] = val
                for k2, v2 in d.clock.items():
                    if kn.get(k2, 0) < v2:
                        kn[k2] = v2
            o.waits = list(waits.items())
            nwaits += len(o.waits)
            if o.lane is not None:
                lane_count[o.lane] = lane_count.get(o.lane, 0) + 16
                o.sig = (("lane", o.lane), lane_count[o.lane])
            elif o.signal:
                eng_count[o.eng] += 1
                o.sig = (o.eng, eng_count[o.eng])
            if o.sig is not None:
                o.clock = dict(kn)
                o.clock[o.sig[0]] = o.sig[1]
            else:
                o.clock = None
        self.stats = dict(n_ops=len(self.ops), n_waits=nwaits, eng_count=eng_count,
                          n_lanes=len(lane_count))

        sems = {}
        for e in self.ENGS:
            sems[e] = nc.alloc_semaphore("s_" + e)
        for i, ln in enumerate(lane_count):
            sems[("lane", ln)] = nc.alloc_semaphore("l_%d" % i)

        per_eng = {e: [] for e in self.ENGS}
        for o in self.ops:
            per_eng[o.eng].append(o)

        def emit(ename, e):
            for o in per_eng[ename]:
                for key, val in o.waits:
                    e.wait_ge(sems[key], val)
                if o.fn is None:
                    continue
                ins = o.fn(e)
                if o.sig is not None:
                    ins.then_inc(sems[o.sig[0]], 16 if o.lane is not None else 1)

        with nc.Block() as block:
            @block.tensor
            def _(e):
                emit("pe", e)

            @block.scalar
            def _(e):
                emit("act", e)

            @block.vector
            def _(e):
                emit("dve", e)

            @block.gpsimd
            def _(e):
                emit("pool", e)

            @block.sync
            def _(e):
                emit("sp", e)


class Ring:
    def __init__(self, items):
        self.items = items
        self.i = 0

    def next(self):
        it = self.items[self.i % len(self.items)]
        self.i += 1
        return it


class LiveRing:
    def __init__(self, items):
        self.items = items
        self.live = [False] * len(items)
        self.i = 0

    def next(self, hold=False):
        for _ in range(len(self.items)):
            k = self.i % len(self.items)
            self.i += 1
            if not self.live[k]:
                self.live[k] = hold
                return self.items[k]
        raise RuntimeError("LiveRing exhausted")

    def release(self, buf):
        for k, it in enumerate(self.items):
            if it[1] is buf:
                self.live[k] = False
                return
        raise KeyError(buf.name)


class Group:
    def __init__(self, gi, c0, w, kind, row0, xb0, halo_src, first=False, halo_save=False, hist_out=None):
        self.gi = gi
        self.c0 = c0
        self.w = w
        self.nblk = (w + 127) // 128
        self.bw = min(128, w)
        self.kind = kind
        self.row0 = row0
        self.xb0 = xb0
        self.halo_src = halo_src
        self.first = first
        self.halo_save = halo_save
        self.hist_out = hist_out
        self.small = w < 512


class Tile:
    pass


def build_program(NSEQ=2, SEQ=4096, SAMPLE=True, NLAYER=2, DEBUG=False):
    nc = bass.Bass("TRN2", target_bir_lowering=False)
    P = Prog()
    TT = 1024
    NSW = 64
    TW = TT + NSW

    def din(name, shape):
        return nc.dram_tensor(name, list(shape), F32, kind="ExternalInput").ap()

    def dout(name, shape):
        return nc.dram_tensor(name, list(shape), F32, kind="ExternalOutput").ap()

    xp = din("xp", [NSEQ * SEQ, D])
    xs = din("xs", [NSW, D])
    sst = din("sst", [2, HIST, DG])
    g_mix = din("g_mix", [2, D])
    w_in = din("w_in", [2, D, 1536])
    g_v = din("g_v", [2, DG])
    w_sp = din("w_sp", [2, 8, 128, 128])
    b_sp = din("b_sp", [2, 8, 128])
    w_pool = din("w_pool", [2, 4, 128, 128])
    pool_scale = din("pool_scale", [2, DG])
    g_branch = din("g_branch", [2, D])
    w_out = din("w_out", [2, D, D])
    g_ffn = din("g_ffn", [2, D])
    w_gate = din("w_gate", [2, D, DFF])
    w_up = din("w_up", [2, D, DFF])
    w_down = din("w_down", [2, DFF, D])
    g_final = din("g_final", [1, D])

    yp = dout("yp", [NSEQ * SEQ, D])
    ys = dout("ys", [NSW, D])
    hp = dout("hp", [2, NSEQ, HIST, DG])
    hs = dout("hs", [2, HIST, DG])
    vs = dout("vs", [2, NSW, DG])
    if DEBUG:
        dbg_ya = dout("dbg_ya", [128, 4, 1024])
        dbg_yb = dout("dbg_yb", [128, 4, 1024])
        dbg_ym = nc.dram_tensor("dbg_ym", [128, 8, 1024], BF16, kind="ExternalOutput").ap()
        dbg_x1 = dout("dbg_x1", [128, 8, 1024])
        dbg_x2 = dout("dbg_x2", [128, 8, 1024])
        dbg_d = nc.dram_tensor("dbg_d", [128, 4, 1024], BF16, kind="ExternalOutput").ap()
        dbg_act = nc.dram_tensor("dbg_act", [128, 22, 1024], BF16, kind="ExternalOutput").ap()
        dbg_h2 = nc.dram_tensor("dbg_h2", [128, 8, 1024], BF16, kind="ExternalOutput").ap()

    def sb(name, shape, dt):
        return nc.alloc_sbuf_tensor(name, list(shape), dt)

    def ps(name):
        return nc.alloc_psum_tensor(name, [128, 512], F32)

    xT = sb("xT", [128, KC, TW], F32)
    hT = sb("hT", [128, KC, TW], BF16)
    sqr = sb("sqr", [128, 4, 512], BF16)
    rsd = sb("rsd", [128, 3, 512], F32)
    NSLOT = 4
    wsl = sb("wsl", [128, NSLOT, 4096], BF16)
    ident = sb("ident", [128, 128], F32)
    inv1024 = sb("inv1024", [128, 128], BF16)
    inv512 = sb("inv512", [128, 128], BF16)
    vecT = sb("vecT", [128, 64], F32)
    gvb = sb("gvb", [128, 2, 512], F32)
    biasb = sb("biasb", [128, 2, 4, 128], F32)
    WsT = sb("WsT", [128, 2, 8, 128], BF16)
    wpl = sb("wpl", [128, 2, 4, 128], BF16)
    invc = sb("invc", [128, 4, 16], F32)
    halo = sb("halo", [128, 2, 4, 16], F32)
    halos = sb("halos", [128, 2, 4, 16], F32)
    ssv = sb("ssv", [128, 8], F32)
    rv = sb("rv", [128, 8], F32)
    rv2 = sb("rv2", [128, 8], F32)
    tiny = sb("tiny", [128, 16], F32)
    hst = sb("hst", [16, 512], F32)
    vout = sb("vout", [128, 512], F32)
    fence_cell = sb("fence_cell", [128, 2], F32)

    XBW = 1120
    XB_P = HIST
    XB_S = HIST + TT + HIST
    OFF_VTMP = 0
    OFF_VN = OFF_VTMP + 2048
    OFF_XB = OFF_VN + 2304
    OFF_DT = OFF_XB + 4 * XBW
    OFF_PT = OFF_DT + 2 * TW
    OFF_UT = OFF_PT + 3 * 528
    OFF_TT = OFF_UT + 1024
    OFF_YA = OFF_TT + 1024
    OFF_YB = OFF_YA + 4 * TW
    U_END = OFF_YB + 4 * TW
    OFF_ACT = 0
    OFF_ST = 11 * TW
    assert OFF_ST + 1536 <= U_END
    U = sb("U", [128, U_END], F32)

    def uf(off, n, pat=None, **kw):
        v = U[:, off:off + n]
        if pat:
            v = v.rearrange(pat, **kw)
        return v

    def ub(off, nwords, pat=None, **kw):
        v = U[:, off:off + nwords].bitcast(BF16)
        if pat:
            v = v.rearrange(pat, **kw)
        return v

    vtmp = uf(OFF_VTMP, 2048, "p (r f) -> p r f", r=4)
    vn = ub(OFF_VN, 2304, "p (b f) -> p b f", b=9)
    xbT = uf(OFF_XB, 4 * XBW, "p (c t) -> p c t", c=4)
    dT = ub(OFF_DT, 2 * TW, "p (c t) -> p c t", c=4)
    ptmp = uf(OFF_PT, 3 * 528, "p (r f) -> p r f", r=3)
    utmp = uf(OFF_UT, 1024, "p (r f) -> p r f", r=2)
    ttmp = uf(OFF_TT, 1024, "p (r f) -> p r f", r=2)
    yaT = uf(OFF_YA, 4 * TW, "p (c t) -> p c t", c=4)
    ybT = uf(OFF_YB, 4 * TW, "p (c t) -> p c t", c=4)
    actT = ub(OFF_ACT, 11 * TW, "p (c t) -> p c t", c=FC)
    stmp = uf(OFF_ST, 1536, "p (r f) -> p r f", r=3)
    NSTG = 8
    stg = uf(0, NSTG * 1024, "p (s f) -> p s f", s=NSTG)

    mm_banks = [ps("mm%d" % i) for i in range(4)]
    aux_banks = [ps("aux%d" % i) for i in range(4)]

    tokU = Buf("tokU")
    bx = [[Buf("x%d_%d" % (c, g)) for g in range(3)] for c in range(KC)]
    bh = [[Buf("h%d_%d" % (c, g)) for g in range(3)] for c in range(KC)]
    bxb = [[Buf("xb%d_%d" % (c, g), tokU) for g in range(3)] for c in range(4)]
    bxbh = Buf("xbh", tokU)
    bxbh2 = Buf("xbh2", tokU)
    bd = [[Buf("d%d_%d" % (c, g), tokU) for g in range(3)] for c in range(4)]
    bya = [[Buf("ya%d_%d" % (c, g), tokU) for g in range(3)] for c in range(4)]
    byb = [[Buf("yb%d_%d" % (c, g), tokU) for g in range(3)] for c in range(4)]
    bvn = [Buf("vn%d" % b, tokU) for b in range(9)]
    bact = [[Buf("a%d_%d" % (f, g), tokU) for g in range(3)] for f in range(FC)]
    MM = Ring([(mm_banks[i], Buf("mm%d" % i)) for i in range(4)])
    AUX = LiveRing([(aux_banks[i], Buf("aux%d" % i)) for i in range(4)])
    TR = AUX
    SQ = Ring([(i, Buf("sq%d" % i)) for i in range(4)])
    RS = Ring([(i, Buf("rs%d" % i)) for i in range(3)])
    STG = Ring([(i, Buf("stg%d" % i, tokU)) for i in range(8)])
    VT = Ring([(i, Buf("vt%d" % i, tokU)) for i in range(4)])
    UT = Ring([(i, Buf("ut%d" % i, tokU)) for i in range(2)])
    TTR = Ring([(i, Buf("tt%d" % i, tokU)) for i in range(2)])
    STM = Ring([(i, Buf("sm%d" % i, tokU)) for i in range(3)])
    SSV = Ring([(i, Buf("ssv%d" % i)) for i in range(2)])
    bpt = [Buf("pt%d" % i, tokU) for i in range(3)]
    btiny = Buf("tiny")
    bident = Buf("ident")
    binv = Buf("inv")
    bvec = Buf("vec")
    bgvb = Buf("gvb")
    bbias = Buf("bias")
    bwst = Buf("wst")
    bwpl = Buf("wpl")
    binvc = Buf("invc")
    bhalo = [Buf("halo%d" % l) for l in range(2)]
    bhalos = [Buf("halos%d" % l) for l in range(2)]
    bhst = Buf("hst")
    bvout = Buf("vout")
    bws = [Buf("ws%d" % i) for i in range(NSLOT)]
    bfence = Buf("fencecell")

    def gmix(l, c):
        return vecT[:, l * 8 + c:l * 8 + c + 1]

    def gffn(l, c):
        return vecT[:, 16 + l * 8 + c:16 + l * 8 + c + 1]

    def gbr(l, c):
        return vecT[:, 32 + l * 8 + c:32 + l * 8 + c + 1]

    def gfin(c):
        return vecT[:, 48 + c:48 + c + 1]

    def psc(l, g):
        return vecT[:, 56 + l * 4 + g:56 + l * 4 + g + 1]

    def mm_op(out, lhsT, rhs, start, stop, reads, writes):
        P.op("pe", lambda e: e.matmul(out, lhsT=lhsT, rhs=rhs, start=start, stop=stop), reads, writes)

    def tr_op(out, in_, idn, reads, writes):
        P.op("pe", lambda e: e.transpose(out, in_, idn), reads + [bident], writes)

    def act_op(out, in_, func, reads, writes, **kw):
        P.op("act", lambda e: e.activation(out=out, in_=in_, func=func, **kw), reads, writes)

    def dve_tt(out, in0, in1, op, reads, writes):
        P.op("dve", lambda e: e.tensor_tensor(out=out, in0=in0, in1=in1, op=op), reads, writes)

    def dve_stt(out, in0, scalar, in1, op0, op1, reads, writes):
        P.op("dve", lambda e: e.scalar_tensor_tensor(out=out, in0=in0, scalar=scalar, in1=in1,
                                                     op0=op0, op1=op1), reads, writes)

    def pool_tt(out, in0, in1, op, reads, writes):
        P.op("pool", lambda e: e.tensor_tensor(out=out, in0=in0, in1=in1, op=op), reads, writes)

    def pool_stt(out, in0, scalar, in1, op0, op1, reads, writes):
        P.op("pool", lambda e: e.scalar_tensor_tensor(out=out, in0=in0, scalar=scalar, in1=in1,
                                                      op0=op0, op1=op1), reads, writes)

    def dve_copy(out, in_, reads, writes):
        P.op("dve", lambda e: e.tensor_copy(out, in_), reads, writes)

    def dve_recip(out, in_, reads, writes):
        P.op("dve", lambda e: e.reciprocal(out, in_), reads, writes)

    def dve_memset(ap, val, writes):
        P.op("dve", lambda e: e.memset(ap, val), [], writes)

    def dma(eng, out, in_, reads, writes, lane, is_out=False):
        P.op(eng, lambda e: e.dma_start(out=out, in_=in_), reads, writes, lane=lane, is_out=is_out)

    def fence():
        P.op("dve", lambda e: e.memset(fence_cell[:, 0:1], 0.0), [], [tokU, bfence])

    wlist = []
    tiles = []
    for s_ in range(NSEQ):
        nt = SEQ // TT
        for ti in range(nt):
            t = Tile()
            first = ti == 0
            last = ti == nt - 1
            r0 = s_ * SEQ + ti * TT
            t.groups = [
                Group(0, 0, 512, "p", r0, XB_P, "zero" if first else "halo", first=first),
                Group(1, 512, 512, "p", r0 + 512, XB_P + 512, "prev", halo_save=not last,
                      hist_out=("p", s_) if last else None),
            ]
            tiles.append(t)
    if SAMPLE:
        tiles[-1].groups.append(Group(2, TT, NSW, "s", 0, XB_S, "halos", hist_out=("s",)))

    for t in tiles:
        for l in range(NLAYER):
            wlist.append(("win", l, 512, 512))
            wlist.append(("win", l, 1024, 512))
            wlist.append(("win", l, 0, 512))
            wlist.append(("wout", l, 0, 512))
            wlist.append(("wout", l, 512, 512))
            for j in range(6):
                nco = 512 if j < 5 else 256
                wlist.append(("gate", l, j * 512, nco))
                wlist.append(("up", l, j * 512, nco))
            for o in range(8):
                wlist.append(("down", l, o * 128, 128))
    wstate = {"next_load": 0, "next_use": 0}
    wsrc = {"win": w_in, "wout": w_out, "gate": w_gate, "up": w_up, "down": w_down}

    def w_view(slot, kind, nco):
        if kind == "down":
            return wsl[:, slot, 0:FC * 128].rearrange("p (k f) -> p k f", k=FC)
        return wsl[:, slot, 0:KC * nco].rearrange("p (k f) -> p k f", k=KC)

    def w_emit_load(i):
        kind, l, c0, nco = wlist[i]
        slot = i % NSLOT
        src = wsrc[kind][l].rearrange("(k p) f -> p k f", p=128)[:, :, c0:c0 + nco]
        dma("pool", w_view(slot, kind, nco), src, [], [bws[slot]], lane=("w", slot))

    def w_acquire(kind, l, c0, hold=0):
        j = wstate["next_use"]
        assert wlist[j][:3] == (kind, l, c0), (wlist[j], kind, l, c0)
        while wstate["next_load"] < min(j - hold + NSLOT, len(wlist)):
            w_emit_load(wstate["next_load"])
            wstate["next_load"] += 1
        wstate["next_use"] += 1
        slot = j % NSLOT
        return w_view(slot, kind, wlist[j][3]), bws[slot]

    P.op("pool", lambda e: e.memset(ident[:], 0.0), [], [bident])
    P.op("pool", lambda e: e.affine_select(out=ident[:], in_=ident[:], pattern=[[-1, 128]],
                                           compare_op=ALU.not_equal, fill=1.0, base=0,
                                           channel_multiplier=1), [bident], [bident])
    dve_memset(inv1024[:], 1.0 / 1024.0, [binv])
    dve_memset(inv512[:], 1.0 / 512.0, [binv])
    dve_memset(fence_cell[:], 0.0, [bfence])
    for g, wd in enumerate(WINDOWS):
        dve_memset(invc[:, g, :], 1.0 / wd, [binvc])
        for tcol in range(wd - 1):
            dve_memset(invc[:, g, tcol:tcol + 1], 1.0 / (tcol + 1), [binvc])
    dve_memset(halo[:], 0.0, [bhalo[0], bhalo[1]])

    s0, bs0 = STG.next()
    vrows = stg[0:64, s0, 0:128]
    dma("sp", stg[0:16, s0, 0:128], g_mix.rearrange("l (c p) -> (l c) p", p=128), [], [bs0], ("c", 0))
    dma("sp", stg[16:32, s0, 0:128], g_ffn.rearrange("l (c p) -> (l c) p", p=128), [], [bs0], ("c", 1))
    dma("sp", stg[32:48, s0, 0:128], g_branch.rearrange("l (c p) -> (l c) p", p=128), [], [bs0], ("c", 2))
    dma("sp", stg[48:56, s0, 0:128], g_final.rearrange("l (c p) -> (l c) p", p=128), [], [bs0], ("c", 3))
    dma("sp", stg[56:64, s0, 0:128], pool_scale.rearrange("l (c p) -> (l c) p", p=128), [], [bs0], ("c", 0))
    tr, btr = TR.next()
    tr_op(tr[:, 0:64], vrows, ident[0:64, 0:64], [bs0], [btr])
    dve_copy(vecT[:, :], tr[:, 0:64], [btr], [bvec])
    for l in range(2):
        dma("sp", gvb[:, l, :], g_v[l:l + 1, :].partition_broadcast(128), [], [bgvb], ("c", 1))
        for c in range(4):
            for half in range(2):
                hh = 2 * c + half
                dma("sp", biasb[half * 64:(half + 1) * 64, l, c, :],
                    b_sp[l, hh:hh + 1, :].partition_broadcast(64), [], [bbias], ("c", 2 + half))
    for l in range(2):
        s1, bs1 = STG.next()
        wsn = stg[:, s1, :].rearrange("p (h j) -> p h j", h=8)
        dma("sp", wsn, w_sp[l].rearrange("h i j -> i h j"), [], [bs1], ("c", 0))
        dve_memset(wsn[0:64, :, 64:128], 0.0, [bs1])
        for half in range(2):
            tr, btr = TR.next()
            trv = tr[:, :].rearrange("p (j f) -> p j f", j=4)
            for j in range(4):
                tr_op(trv[:, j, :], wsn[:, half * 4 + j, :], ident[:, :], [bs1], [btr])
            dve_copy(WsT[:, l, half * 4:half * 4 + 4, :], trv, [btr], [bwst])
        dma("pool", wpl[:, l, :, :], w_pool[l].rearrange("g c d -> c g d"), [], [bwpl], ("c", 4))
    if SAMPLE:
        for l in range(2):
            s1, bs1 = STG.next()
            dma("sp", stg[0:HIST, s1, 0:512], sst[l], [], [bs1], ("c", 1))
            tr, btr = TR.next()
            trv = tr[:, :].rearrange("p (j f) -> p j f", j=4)
            for c in range(4):
                tr_op(trv[:, c, 0:HIST], stg[0:HIST, s1, c * 128:(c + 1) * 128], ident[0:HIST, 0:HIST],
                      [bs1], [btr])
            dve_copy(halos[:, l, :, 0:HIST], trv[:, :, 0:HIST], [btr], [bhalos[l]])
    for b in (bident, binv, bvec, bgvb, bbias, bwst, bwpl, binvc):
        b.const = True

    pend = []

    def flush():
        for f in pend:
            f()
        del pend[:]

    def cols(g):
        return slice(g.c0, g.c0 + g.w)

    def stats_sq(g, src_ap, src_buf, st, bst, first, last, invt, defer=True):
        w = g.w
        s, bs = SQ.next()
        act_op(sqr[:, s, 0:w], src_ap, AF.Square, [src_buf], [bs])

        def f():
            mm_op(st[:, 0:w], invt[:, :], sqr[:, s, 0:w], first, last, [bs, binv], [bst])
        if defer:
            pend.append(f)
        else:
            f()

    def rstd_finish(g, st, bst):
        w = g.w
        r, br = RS.next()
        act_op(rsd[:, r, 0:w], st[:, 0:w], AF.Ln, [bst], [br], bias=eps_ap[:, 0:1])
        act_op(rsd[:, r, 0:w], rsd[:, r, 0:w], AF.Exp, [br], [br], scale=-0.5)
        AUX.release(bst)
        return r, br

    def norm_apply(g, src_ap, src_buf, n, r, br, gcol, dst_ap, dst_buf):
        w = g.w
        for c in range(n):
            dve_stt(dst_ap(c), src_ap(c), gcol(c), rsd[:, r, 0:w], ALU.mult, ALU.mult,
                    [src_buf(c), br, bvec], [dst_buf(c)])

    def prep_norm(g, st, gcol, dst_ap, dst_buf):
        flush()
        r, br = rstd_finish(g, *st)
        norm_apply(g, lambda c: xT[:, c, cols(g)], lambda c: bx[c][g.gi], KC, r, br, gcol, dst_ap, dst_buf)

    def prep_norm1(l):
        return lambda g, st: prep_norm(g, st, lambda c: gmix(l, c), lambda c: hT[:, c, cols(g)],
                                       lambda c: bh[c][g.gi])

    def prep_ffn(l):
        return lambda g, st: prep_norm(g, st, lambda c: gffn(l, c), lambda c: hT[:, c, cols(g)],
                                       lambda c: bh[c][g.gi])

    def prep_final():
        return lambda g, st: prep_norm(g, st, lambda c: gfin(c), lambda c: xT[:, c, cols(g)],
                                       lambda c: bx[c][g.gi])

    def full_stats(g):
        st, bst = AUX.next(hold=True)
        for c in range(KC):
            stats_sq(g, xT[:, c, cols(g)], bx[c][g.gi], st, bst, c == 0, c == KC - 1, inv1024, defer=False)
        return st, bst

    def load_block(t, g, b):
        src = xp if g.kind == "p" else xs
        bw = g.bw
        r0 = g.row0 + b * 128
        s, bs = STG.next()
        dma("pool", stg[0:bw, s, :], src[r0:r0 + bw, :], [], [bs], ("ld", s))
        col = g.c0 + b * 128
        for half in range(2):
            tr, btr = AUX.next()
            trv = tr[:, :].rearrange("p (j f) -> p j f", j=4)
            for j in range(4):
                c = half * 4 + j
                tr_op(trv[:, j, 0:bw], stg[0:bw, s, c * 128:(c + 1) * 128], ident[0:bw, 0:bw],
                      [bs], [btr])
            dve_copy(xT[:, half * 4:half * 4 + 4, col:col + bw], trv[:, :, 0:bw],
                     [btr], [bx[half * 4 + j][g.gi] for j in range(4)])

    def load_tile(t):
        sts = {}
        for g in t.groups:
            for b in range(g.nblk):
                load_block(t, g, b)
        for g in t.groups:
            sts[g.gi] = full_stats(g)
        return sts

    def mixer(t, l, st_in, done):
        G = t.groups
        flush()
        for g in G:
            if g.gi not in done:
                prep_norm1(l)(g, st_in[g.gi])
        for g in G:
            hcols = slice(g.xb0 - HIST, g.xb0)
            if g.halo_src == "halos":
                dve_copy(xbT[:, :, hcols], halos[:, l, :, 0:HIST], [bhalos[l]], [bxbh2])
            elif g.halo_src == "zero":
                dve_memset(xbT[:, :, hcols], 0.0, [bxbh])
            elif g.halo_src == "halo":
                dve_copy(xbT[:, :, hcols], halo[:, l, :, 0:HIST], [bhalo[l]], [bxbh])
        wv, bwv = w_acquire("win", l, 512)
        wx, bwx = w_acquire("win", l, 1024, hold=1)
        for g in G:
            sv, bsv = SSV.next()
            bw = g.bw
            nb_ = g.nblk
            vts = []
            for b in range(nb_):
                col = g.c0 + b * 128
                mm, bmm = MM.next()
                for k in range(KC):
                    mm_op(mm[0:bw, :], hT[:, k, col:col + bw], wv[:, k, :], k == 0, k == KC - 1,
                          [bh[k][g.gi], bwv], [bmm])
                vt, bvt = VT.next()
                vts.append((vt, bvt))
                act_op(vtmp[0:bw, vt, :], mm[0:bw, :], AF.Gelu, [bmm], [bvt])
                s, bs = SQ.next()
                act_op(sqr[0:bw, s, :], vtmp[0:bw, vt, :], AF.Square, [bvt], [bs, bsv],
                       accum_out=ssv[0:bw, sv * 4 + b:sv * 4 + b + 1])
            P.op("dve", lambda e, sv=sv, bw=bw, nb_=nb_: e.tensor_scalar(
                out=rv[0:bw, sv * 4:sv * 4 + nb_], in0=ssv[0:bw, sv * 4:sv * 4 + nb_], scalar1=1.0 / DG,
                scalar2=EPS, op0=ALU.mult, op1=ALU.add), [bsv], [bsv])
            act_op(rv[0:bw, sv * 4:sv * 4 + nb_], rv[0:bw, sv * 4:sv * 4 + nb_], AF.Ln, [bsv], [bsv])
            act_op(rv2[0:bw, sv * 4:sv * 4 + nb_], rv[0:bw, sv * 4:sv * 4 + nb_], AF.Exp, [bsv], [bsv],
                   scale=-0.5)
            for b in range(nb_):
                vt, bvt = vts[b]
                bi = g.gi * 4 + b
                rcol = rv2[0:bw, sv * 4 + b:sv * 4 + b + 1]
                if g.kind == "s":
                    dve_stt(vout[0:bw, :], vtmp[0:bw, vt, :], rcol, gvb[0:bw, l, :],
                            ALU.mult, ALU.mult, [bvt, bsv, bgvb], [bvout])
                    dve_copy(vn[0:bw, bi, :], vout[0:bw, :], [bvout], [bvn[bi]])
                    dma("sp", vs[l, 0:bw, :], vout[0:bw, :], [bvout], [], ("vout", 0), is_out=True)
                else:
                    dve_stt(vn[0:bw, bi, :], vtmp[0:bw, vt, :], rcol, gvb[0:bw, l, :],
                            ALU.mult, ALU.mult, [bvt, bsv, bgvb], [bvn[bi]])
            for c in range(4):
                mm, bmm = MM.next()
                for k in range(KC):
                    mm_op(mm[:, 0:g.w], wx[:, k, c * 128:(c + 1) * 128], hT[:, k, cols(g)],
                          k == 0, k == KC - 1, [bh[k][g.gi], bwx], [bmm])
                act_op(xbT[:, c, g.xb0:g.xb0 + g.w], mm[:, 0:g.w], AF.Copy,
                       [bmm], [bxb[c][g.gi]])
        for gl in G:
            if gl.halo_save:
                dve_copy(halo[:, l, :, 0:HIST], xbT[:, :, gl.xb0 + gl.w - HIST:gl.xb0 + gl.w],
                         [bxb[c][gl.gi] for c in range(4)], [bhalo[l]])
        for gl in G:
            if gl.hist_out is None:
                continue
            tr, btr = AUX.next()
            trv = tr[:, :].rearrange("p (j f) -> p j f", j=4)
            for c in range(4):
                tr_op(trv[0:HIST, c, :], xbT[:, c, gl.xb0 + gl.w - HIST:gl.xb0 + gl.w], ident[:, :],
                      [bxb[c][gl.gi]], [btr])
            dve_copy(hst[0:HIST, :].rearrange("p (j f) -> p j f", j=4), trv[0:HIST, :, :], [btr], [bhst])
            dst = hp[l, gl.hist_out[1]] if gl.hist_out[0] == "p" else hs[l]
            dma("sp", dst, hst[0:HIST, :], [bhst], [], ("hst", 0), is_out=True)
        def pool_dve(g, gi):
            w = g.w
            wd = WINDOWS[gi]
            if True:
                base = g.xb0

                def X(a, b):
                    return xbT[:, gi, base + a:base + b]

                prevb = {"prev": bxb[gi][g.gi - 1], "halos": bxbh2}.get(g.halo_src, bxbh)
                rd = [bxb[gi][g.gi], prevb]
                ext = wd - 2
                pool_tt(ptmp[:, 0, 0:w + ext], X(-ext, w), X(-ext - 1, w - 1), ALU.add, rd, [bpt[0]])
                cur = 0
                span = 2
                while span < wd:
                    ext2 = wd - 2 * span
                    nxt = (cur + 1) % 3
                    pool_tt(ptmp[:, nxt, 0:w + ext2], ptmp[:, cur, span:span + w + ext2],
                           ptmp[:, cur, 0:w + ext2], ALU.add, [bpt[cur]], [bpt[nxt]])
                    cur = nxt
                    span *= 2
                nx2 = (cur + 1) % 3
                pool_tt(ptmp[:, nx2, 0:w], ptmp[:, cur, 0:w], invc[:, gi, 15:16].to_broadcast([128, w]),
                        ALU.mult, [bpt[cur], binvc], [bpt[nx2]])
                pool_tt(dT[:, gi, cols(g)], ptmp[:, nx2, 0:w], X(0, w), ALU.subtract,
                        [bpt[nx2], bxb[gi][g.gi]], [bd[gi][g.gi]])
                if g.first:
                    pool_tt(tiny[:, 0:16], ptmp[:, cur, 0:16], invc[:, gi, :], ALU.mult,
                           [bpt[cur], binvc], [btiny])
                    pool_tt(dT[:, gi, g.c0:g.c0 + 16], tiny[:, 0:16], X(0, 16), ALU.subtract,
                           [btiny, bxb[gi][g.gi]], [bd[gi][g.gi]])
        for g in G:
            for gi in range(4):
                pool_dve(g, gi)
        wu, bwu = w_acquire("win", l, 0)
        sts_a = {}
        sts_b = {}
        pool_q = []

        def pool_mm(g, gi, stb, bstb):
            w = g.w
            mm, bmm = MM.next()
            mm_op(mm[:, 0:w], wpl[:, l, gi, :], dT[:, gi, cols(g)], True, True,
                  [bd[gi][g.gi], bwpl], [bmm])
            flush()
            act_op(ybT[:, gi, cols(g)], mm[:, 0:w], AF.Copy, [bmm, bvec], [byb[gi][g.gi]],
                   scale=psc(l, gi))
            if not g.small:
                stats_sq(g, ybT[:, gi, cols(g)], byb[gi][g.gi], stb, bstb, gi == 0, gi == 3, inv512)

        def pool_drain(g=None, n=None):
            k = 0
            while pool_q and (g is None or pool_q[0][0] is g) and (n is None or k < n):
                pool_q.pop(0)[1]()
                k += 1

        for g in G:
            w = g.w
            bw = g.bw
            if g.small:
                sta = bsta = stb = bstb = None
            else:
                sta, bsta = AUX.next(hold=True)
                stb, bstb = AUX.next(hold=True)
                sts_a[g.gi] = (sta, bsta)
                sts_b[g.gi] = (stb, bstb)
            for c in range(4):
                mmx, bmx = MM.next()
                for half in range(2):
                    hh = 2 * c + half
                    for b in range(g.nblk):
                        bi = g.gi * 4 + b
                        mm_op(mmx[half * 64:(half + 1) * 64, b * 128:b * 128 + bw],
                              vn[0:bw, bi, hh * 64:(hh + 1) * 64], WsT[0:bw, l, hh, 0:bw], True, True,
                              [bvn[bi], bwst], [bmx])
                mmy, bmy = MM.next()
                for k in range(KC):
                    mm_op(mmy[:, 0:w], wu[:, k, c * 128:(c + 1) * 128], hT[:, k, cols(g)],
                          k == 0, k == KC - 1, [bh[k][g.gi], bwu], [bmy])
                flush()
                pool_drain(n=1)
                ut, but = UT.next()
                act_op(utmp[:, ut, 0:w], mmy[:, 0:w], AF.Gelu, [bmy], [but])
                tt, btt = TTR.next()
                if g.nblk > 1:
                    dve_tt(ttmp[:, tt, 0:w].rearrange("p (b i) -> p b i", b=g.nblk),
                           mmx[:, 0:w].rearrange("p (b i) -> p b i", b=g.nblk),
                           biasb[:, l, c, :].unsqueeze(1).to_broadcast([128, g.nblk, 128]),
                           ALU.add, [bmx, bbias], [btt])
                else:
                    dve_tt(ttmp[:, tt, 0:w], mmx[:, 0:w], biasb[:, l, c, 0:w], ALU.add, [bmx, bbias], [btt])
                dve_tt(yaT[:, c, cols(g)], ttmp[:, tt, 0:w], utmp[:, ut, 0:w], ALU.mult,
                       [btt, but], [bya[c][g.gi]])
                if not g.small:
                    stats_sq(g, yaT[:, c, cols(g)], bya[c][g.gi], sta, bsta, c == 0, c == 3, inv512)
            for gi in range(4):
                pool_q.append((g, lambda g=g, gi=gi, stb=stb, bstb=bstb: pool_mm(g, gi, stb, bstb)))
        wo = [None, None]
        bwo = [None, None]
        wo[0], bwo[0] = w_acquire("wout", l, 0)
        wo[1], bwo[1] = w_acquire("wout", l, 512, hold=1)
        st_out = {}
        done_ffn = set()

        def prep_wout(g):
            pool_drain(g=g)
            flush()
            if g.small:
                sta, bsta = AUX.next(hold=True)
                for c in range(4):
                    stats_sq(g, yaT[:, c, cols(g)], bya[c][g.gi], sta, bsta, c == 0, c == 3, inv512, defer=False)
                ra, bra = rstd_finish(g, sta, bsta)
                stb, bstb = AUX.next(hold=True)
                for c in range(4):
                    stats_sq(g, ybT[:, c, cols(g)], byb[c][g.gi], stb, bstb, c == 0, c == 3, inv512, defer=False)
                rb, brb = rstd_finish(g, stb, bstb)
            else:
                ra, bra = rstd_finish(g, *sts_a[g.gi])
                rb, brb = rstd_finish(g, *sts_b[g.gi])
            norm_apply(g, lambda c: yaT[:, c, cols(g)], lambda c: bya[c][g.gi], 4, ra, bra,
                       lambda c: gbr(l, c), lambda c: hT[:, c, cols(g)], lambda c: bh[c][g.gi])
            norm_apply(g, lambda c: ybT[:, c, cols(g)], lambda c: byb[c][g.gi], 4, rb, brb,
                       lambda c: gbr(l, 4 + c), lambda c: hT[:, 4 + c, cols(g)], lambda c: bh[4 + c][g.gi])

        prep_wout(G[0])
        for k, g in enumerate(G):
            st, bst = AUX.next(hold=True)
            st_out[g.gi] = (st, bst)
            u = 0
            for ch in range(2):
                for oc in range(4):
                    o = ch * 4 + oc
                    mm, bmm = MM.next()
                    for kk in range(KC):
                        mm_op(mm[:, 0:g.w], wo[ch][:, kk, oc * 128:(oc + 1) * 128], hT[:, kk, cols(g)],
                              kk == 0, kk == KC - 1, [bh[kk][g.gi], bwo[ch]], [bmm])
                    flush()
                    pool_drain(n=1)
                    dve_tt(xT[:, o, cols(g)], xT[:, o, cols(g)], mm[:, 0:g.w], ALU.add,
                           [bx[o][g.gi], bmm], [bx[o][g.gi]])
                    stats_sq(g, xT[:, o, cols(g)], bx[o][g.gi], st, bst, o == 0, o == 7, inv1024)
                    u += 1
                    if u == 2 and k >= 1:
                        pg = G[k - 1]
                        prep_ffn(l)(pg, st_out[pg.gi])
                        done_ffn.add(pg.gi)
                    if u == 5 and k + 1 < len(G):
                        prep_wout(G[k + 1])
        return st_out, done_ffn

    def ffn(t, l, st_in, done, next_prep):
        G = t.groups
        for g in G:
            if g.gi not in done:
                prep_ffn(l)(g, st_in[g.gi])
        for j in range(6):
            nco = 512 if j < 5 else 256
            wg, bwg = w_acquire("gate", l, j * 512)
            wu, bwu = w_acquire("up", l, j * 512, hold=1)
            for g in G:
                w = g.w
                for fc in range(nco // 128):
                    f = j * 4 + fc
                    ma, bma = MM.next()
                    for k in range(KC):
                        mm_op(ma[:, 0:w], wg[:, k, fc * 128:(fc + 1) * 128], hT[:, k, cols(g)],
                              k == 0, k == KC - 1, [bh[k][g.gi], bwg], [bma])
                    mb, bmb = MM.next()
                    for k in range(KC):
                        mm_op(mb[:, 0:w], wu[:, k, fc * 128:(fc + 1) * 128], hT[:, k, cols(g)],
                              k == 0, k == KC - 1, [bh[k][g.gi], bwu], [bmb])
                    sm, bsm = STM.next()
                    act_op(stmp[:, sm, 0:w], ma[:, 0:w], AF.Silu, [bma], [bsm])
                    dve_tt(actT[:, f, cols(g)], stmp[:, sm, 0:w], mb[:, 0:w], ALU.mult,
                           [bsm, bmb], [bact[f][g.gi]])
        st_out = {}
        done_next = set()
        for g in G:
            st_out[g.gi] = AUX.next(hold=True)
        for op_ in range(4):
            wd_ = [None, None]
            bwd = [None, None]
            wd_[0], bwd[0] = w_acquire("down", l, (2 * op_) * 128)
            wd_[1], bwd[1] = w_acquire("down", l, (2 * op_ + 1) * 128, hold=1)
            for kg, g in enumerate(G):
                st, bst = st_out[g.gi]
                for i2 in range(2):
                    o = 2 * op_ + i2
                    mm, bmm = MM.next()
                    for k in range(FC):
                        mm_op(mm[:, 0:g.w], wd_[i2][:, k, :], actT[:, k, cols(g)], k == 0, k == FC - 1,
                              [bact[k][g.gi], bwd[i2]], [bmm])
                    flush()
                    dve_tt(xT[:, o, cols(g)], xT[:, o, cols(g)], mm[:, 0:g.w], ALU.add,
                           [bx[o][g.gi], bmm], [bx[o][g.gi]])
                    stats_sq(g, xT[:, o, cols(g)], bx[o][g.gi], st, bst, o == 0, o == 7, inv1024)
                    if op_ == 3 and i2 == 1 and kg >= 1:
                        pg = G[kg - 1]
                        next_prep(pg, st_out[pg.gi])
                        done_next.add(pg.gi)
        return st_out, done_next

    def final_block(t, g, b):
        dst = yp if g.kind == "p" else ys
        bw = g.bw
        col = g.c0 + b * 128
        r0 = g.row0 + b * 128
        s, bs = STG.next()
        for half in range(2):
            tr, btr = AUX.next()
            trv = tr[:, :].rearrange("p (j f) -> p j f", j=4)
            for j in range(4):
                c = half * 4 + j
                tr_op(trv[0:bw, j, :], xT[:, c, col:col + bw], ident[:, :], [bx[c][g.gi]], [btr])
            act_op(stg[0:bw, s, half * 512:(half + 1) * 512].rearrange("p (j f) -> p j f", j=4),
                   trv[0:bw, :, :], AF.Copy, [btr], [bs])
        dma("sp", dst[r0:r0 + bw, :], stg[0:bw, s, :], [bs], [], ("stg", s), is_out=True)

    def final_and_load(t, st_in, nxt, done):
        for g in t.groups:
            if g.gi not in done:
                prep_final()(g, st_in[g.gi])
        ngroups = {g.gi: g for g in (nxt.groups if nxt is not None else [])}
        pendL = []
        for g in t.groups:
            for b in range(g.nblk):
                final_block(t, g, b)
                if pendL:
                    ng, nb = pendL.pop(0)
                    load_block(nxt, ng, nb)
            if g.gi in ngroups:
                ng = ngroups.pop(g.gi)
                pendL += [(ng, nb) for nb in range(ng.nblk)]
        for ng, nb in pendL:
            load_block(nxt, ng, nb)
        for ng in ngroups.values():
            for nb in range(ng.nblk):
                load_block(nxt, ng, nb)
        sts = {}
        if nxt is not None:
            for g in nxt.groups:
                sts[g.gi] = full_stats(g)
        return sts

    eps_ap = sb("eps_ap", [128, 1], F32)
    beps = Buf("eps")
    dve_memset(eps_ap[:], EPS, [beps])
    beps.const = True
    P.op("act", lambda e: e.activation(out=fence_cell[:, 1:2], in_=eps_ap[:, 0:1], func=AF.Copy),
         [beps], [bfence])

    st = load_tile(tiles[0])
    for ti, t in enumerate(tiles):
        done = set()
        for l in range(NLAYER):
            fence()
            st, done = mixer(t, l, st, done)
            if DEBUG and t is tiles[0] and l == 0:
                dma("sp", dbg_x1, xT[:, :, :], [b2 for row in bx for b2 in row], [], ("dbg", 4), is_out=True)
            fence()
            nprep = prep_norm1(l + 1) if l + 1 < NLAYER else prep_final()
            st, done = ffn(t, l, st, done, nprep)
        fence()
        st = final_and_load(t, st, tiles[ti + 1] if ti + 1 < len(tiles) else None, done)
    flush()

    assert wstate["next_use"] == len(wlist)
    P.finalize(nc)
    return nc, P


_CACHE = {}


def kernel(x_prompt, x_sample, state_pool, g_mix, w_in, g_v, w_spatial, b_spatial, w_pool, pool_scale,
           g_branch, w_out, g_ffn, w_gate, w_up, w_down, g_final):
    f = lambda a: np.ascontiguousarray(np.asarray(a, dtype=np.float32))
    x_prompt = f(x_prompt)
    x_sample = f(x_sample)
    state_pool = f(state_pool)
    B, S, _ = x_prompt.shape
    nseq = B // N_CORES
    if "nc" not in _CACHE:
        _CACHE["nc"] = build_program(NSEQ=nseq, SEQ=S, SAMPLE=True)[0]
    nc = _CACHE["nc"]
    shared = {
        "g_mix": f(g_mix), "w_in": f(w_in), "g_v": f(g_v), "w_sp": f(w_spatial), "b_sp": f(b_spatial),
        "w_pool": f(w_pool), "pool_scale": f(pool_scale), "g_branch": f(g_branch), "w_out": f(w_out),
        "g_ffn": f(g_ffn), "w_gate": f(w_gate), "w_up": f(w_up), "w_down": f(w_down),
        "g_final": f(g_final).reshape(1, D),
    }
    in_maps = []
    for i in range(N_CORES):
        m = dict(shared)
        m["xp"] = x_prompt[i * nseq:(i + 1) * nseq].reshape(nseq * S, D)
        m["xs"] = x_sample[i]
        m["sst"] = np.ascontiguousarray(state_pool[:, i])
        in_maps.append(m)
    res = run_bass_kernel_spmd(nc, in_maps, core_ids=list(range(N_CORES)))
    rs = res.results
    y_prompt = np.concatenate([r["yp"].reshape(nseq, S, D) for r in rs], axis=0)
    y_sample = np.stack([r["ys"] for r in rs], axis=0)
    hp = np.concatenate([r["hp"] for r in rs], axis=1)
    hs = np.stack([r["hs"] for r in rs], axis=1)
    vs = np.stack([r["vs"] for r in rs], axis=1)
    return (y_prompt.astype(np.float32), y_sample.astype(np.float32), hp.astype(np.float32),
            hs.astype(np.float32), vs.astype(np.float32))
```

```python
import numpy as np
import concourse.bass as bass
import concourse.mybir as mybir
from concourse.bass_utils import run_bass_kernel_spmd

F32 = mybir.dt.float32
BF16 = mybir.dt.bfloat16
AF = mybir.ActivationFunctionType
ALU = mybir.AluOpType

EPS = 1e-6
D = 1024
DG = 512
DFF = 2816
KC = 8
FC = 22
HIST = 15
WINDOWS = (2, 4, 8, 16)
N_CORES = 8


class Buf:
    __slots__ = ("name", "w", "rs", "const", "tok")

    def __init__(self, name, tok=None):
        self.name = name
        self.w = None
        self.rs = {}
        self.const = False
        self.tok = tok


class Op:
    __slots__ = ("eng", "fn", "deps", "lane", "signal", "sig", "clock", "waits", "idx")


class Prog:
    ENGS = ("pe", "act", "dve", "pool", "sp")

    def __init__(self):
        self.ops = []
        self.lane_last = {}
        self.out_ops = []

    def op(self, eng, fn, reads=(), writes=(), lane=None, is_out=False):
        o = Op()
        o.eng = eng
        o.fn = fn
        o.lane = lane
        o.signal = lane is not None
        o.idx = len(self.ops)
        o.sig = None
        deps = {}
        reads = list(reads)
        for b in list(reads) + list(writes):
            if b.tok is not None and b.tok not in reads:
                reads.append(b.tok)

        def add(d):
            if d is None:
                return
            if d.lane is None and d.eng == "pe" and eng == "pe" and lane is None:
                return
            key = d.eng if d.lane is None else ("lane", d.lane)
            cur = deps.get(key)
            if cur is None or cur.idx < d.idx:
                deps[key] = d

        for b in reads:
            add(b.w)
        for b in writes:
            add(b.w)
            for r in b.rs.values():
                add(r)
        if lane is not None:
            add(self.lane_last.get(lane))
            self.lane_last[lane] = o
        mykey = eng if lane is None else ("lane", lane)
        for b in reads:
            if not b.const:
                b.rs[mykey] = o
        for b in writes:
            b.w = o
            b.rs = {}
        o.deps = sorted(deps.values(), key=lambda d: d.idx)
        for d in o.deps:
            d.signal = True
        self.ops.append(o)
        if is_out:
            self.out_ops.append(o)
        return o

    def finalize(self, nc):
        fin = Op()
        fin.eng = "sp"
        fin.fn = None
        fin.lane = None
        fin.signal = False
        fin.idx = len(self.ops)
        fin.sig = None
        dd = {}
        for d in self.out_ops:
            key = ("lane", d.lane)
            if key not in dd or dd[key].idx < d.idx:
                dd[key] = d
        fin.deps = sorted(dd.values(), key=lambda d: d.idx)
        self.ops.append(fin)

        eng_count = {e: 0 for e in self.ENGS}
        lane_count = {}
        known = {e: {} for e in self.ENGS}
        nwaits = 0
        for o in self.ops:
            kn = known[o.eng]
            waits = {}
            for d in o.deps:
                key, val = d.sig
                if kn.get(key, 0) >= val:
                    continue
                if waits.get(key, 0) < val:
                    waits[key] = val
                for k2, v2 in d.clock.items():
                    if kn.get(k2, 0) < v2:
                        kn[k2] = v2
            o.waits = list(waits.items())
            nwaits += len(o.waits)
            if o.lane is not None:
                lane_count[o.lane] = lane_count.get(o.lane, 0) + 16
                o.sig = (("lane", o.lane), lane_count[o.lane])
            elif o.signal:
                eng_count[o.eng] += 1
                o.sig = (o.eng, eng_count[o.eng])
            if o.sig is not None:
                o.clock = dict(kn)
                o.clock[o.sig[0]] = o.sig[1]
            else:
                o.clock = None
        self.stats = dict(n_ops=len(self.ops), n_waits=nwaits, eng_count=eng_count,
                          n_lanes=len(lane_count))

        sems = {}
        for e in self.ENGS:
            sems[e] = nc.alloc_semaphore("s_" + e)
        for i, ln in enumerate(lane_count):
            sems[("lane", ln)] = nc.alloc_semaphore("l_%d" % i)

        per_eng = {e: [] for e in self.ENGS}
        for o in self.ops:
            per_eng[o.eng].append(o)

        def emit(ename, e):
            for o in per_eng[ename]:
                for key, val in o.waits:
                    e.wait_ge(sems[key], val)
                if o.fn is None:
                    continue
                ins = o.fn(e)
                if o.sig is not None:
                    ins.then_inc(sems[o.sig[0]], 16 if o.lane is not None else 1)

        with nc.Block() as block:
            @block.tensor
            def _(e):
                emit("pe", e)

            @block.scalar
            def _(e):
                emit("act", e)

            @block.vector
            def _(e):
                emit("dve", e)

            @block.gpsimd
            def _(e):
                emit("pool", e)

            @block.sync
            def _(e):
                emit("sp", e)


class Ring:
    def __init__(self, items):
        self.items = items
        self.i = 0

    def next(self):
        it = self.items[self.i % len(self.items)]
        self.i += 1
        return it


class LiveRing:
    def __init__(self, items):
        self.items = items
        self.live = [False] * len(items)
        self.i = 0

    def next(self, hold=False):
        for _ in range(len(self.items)):
            k = self.i % len(self.items)
            self.i += 1
            if not self.live[k]:
                self.live[k] = hold
                return self.items[k]
        raise RuntimeError("LiveRing exhausted")

    def release(self, buf):
        for k, it in enumerate(self.items):
            if it[1] is buf:
                self.live[k] = False
                return
        raise KeyError(buf.name)


class Group:
    def __init__(self, gi, c0, w, kind, row0, xb0, halo_src, first=False, halo_save=False, hist_out=None):
        self.gi = gi
        self.c0 = c0
        self.w = w
        self.nblk = (w + 127) // 128
        self.bw = min(128, w)
        self.kind = kind
        self.row0 = row0
        self.xb0 = xb0
        self.halo_src = halo_src
        self.first = first
        self.halo_save = halo_save
        self.hist_out = hist_out
        self.small = w < 512


class Tile:
    pass


def build_program(NSEQ=2, SEQ=4096, SAMPLE=True, NLAYER=2, DEBUG=False):
    nc = bass.Bass("TRN2", target_bir_lowering=False)
    P = Prog()
    TT = 1024
    NSW = 64
    TW = TT + NSW

    def din(name, shape):
        return nc.dram_tensor(name, list(shape), F32, kind="ExternalInput").ap()

    def dout(name, shape):
        return nc.dram_tensor(name, list(shape), F32, kind="ExternalOutput").ap()

    xp = din("xp", [NSEQ * SEQ, D])
    xs = din("xs", [NSW, D])
    sst = din("sst", [2, HIST, DG])
    g_mix = din("g_mix", [2, D])
    w_in = din("w_in", [2, D, 1536])
    g_v = din("g_v", [2, DG])
    w_sp = din("w_sp", [2, 8, 128, 128])
    b_sp = din("b_sp", [2, 8, 128])
    w_pool = din("w_pool", [2, 4, 128, 128])
    pool_scale = din("pool_scale", [2, DG])
    g_branch = din("g_branch", [2, D])
    w_out = din("w_out", [2, D, D])
    g_ffn = din("g_ffn", [2, D])
    w_gate = din("w_gate", [2, D, DFF])
    w_up = din("w_up", [2, D, DFF])
    w_down = din("w_down", [2, DFF, D])
    g_final = din("g_final", [1, D])

    yp = dout("yp", [NSEQ * SEQ, D])
    ys = dout("ys", [NSW, D])
    hp = dout("hp", [2, NSEQ, HIST, DG])
    hs = dout("hs", [2, HIST, DG])
    vs = dout("vs", [2, NSW, DG])
    if DEBUG:
        dbg_ya = dout("dbg_ya", [128, 4, 1024])
        dbg_yb = dout("dbg_yb", [128, 4, 1024])
        dbg_ym = nc.dram_tensor("dbg_ym", [128, 8, 1024], BF16, kind="ExternalOutput").ap()
        dbg_x1 = dout("dbg_x1", [128, 8, 1024])
        dbg_x2 = dout("dbg_x2", [128, 8, 1024])
        dbg_d = nc.dram_tensor("dbg_d", [128, 4, 1024], BF16, kind="ExternalOutput").ap()
        dbg_act = nc.dram_tensor("dbg_act", [128, 22, 1024], BF16, kind="ExternalOutput").ap()
        dbg_h2 = nc.dram_tensor("dbg_h2", [128, 8, 1024], BF16, kind="ExternalOutput").ap()

    def sb(name, shape, dt):
        return nc.alloc_sbuf_tensor(name, list(shape), dt)

    def ps(name):
        return nc.alloc_psum_tensor(name, [128, 512], F32)

    xT = sb("xT", [128, KC, TW], F32)
    hT = sb("hT", [128, KC, TW], BF16)
    sqr = sb("sqr", [128, 4, 512], BF16)
    rsd = sb("rsd", [128, 3, 512], F32)
    NSLOT = 4
    wsl = sb("wsl", [128, NSLOT, 4096], BF16)
    ident = sb("ident", [128, 128], F32)
    inv1024 = sb("inv1024", [128, 128], BF16)
    inv512 = sb("inv512", [128, 128], BF16)
    vecT = sb("vecT", [128, 64], F32)
    gvb = sb("gvb", [128, 2, 512], F32)
    biasb = sb("biasb", [128, 2, 4, 128], F32)
    WsT = sb("WsT", [128, 2, 8, 128], BF16)
    wpl = sb("wpl", [128, 2, 4, 128], BF16)
    invc = sb("invc", [128, 4, 16], F32)
    halo = sb("halo", [128, 2, 4, 16], F32)
    halos = sb("halos", [128, 2, 4, 16], F32)
    ssv = sb("ssv", [128, 8], F32)
    rv = sb("rv", [128, 8], F32)
    rv2 = sb("rv2", [128, 8], F32)
    tiny = sb("tiny", [128, 16], F32)
    vout = sb("vout", [128, 512], F32)
    hst = vout
    fence_cell = sb("fence_cell", [128, 2], F32)

    XBW = 1120
    XB_P = HIST
    XB_S = HIST + TT + HIST
    OFF_VTMP = 0
    OFF_VN = OFF_VTMP + 2048
    OFF_XB = OFF_VN + 2304
    OFF_DT = OFF_XB + 4 * XBW
    OFF_PT = OFF_DT + 2 * TW
    OFF_UT = OFF_PT + 4 * 528
    OFF_TT = OFF_UT + 1024
    OFF_YA = OFF_TT + 1024
    OFF_YB = OFF_YA + 4 * TW
    U_END = OFF_YB + 4 * TW
    OFF_ACT = 0
    OFF_ST = 11 * TW
    assert OFF_ST + 1536 <= U_END
    U = sb("U", [128, U_END], F32)

    def uf(off, n, pat=None, **kw):
        v = U[:, off:off + n]
        if pat:
            v = v.rearrange(pat, **kw)
        return v

    def ub(off, nwords, pat=None, **kw):
        v = U[:, off:off + nwords].bitcast(BF16)
        if pat:
            v = v.rearrange(pat, **kw)
        return v

    vtmp = uf(OFF_VTMP, 2048, "p (r f) -> p r f", r=4)
    vn = ub(OFF_VN, 2304, "p (b f) -> p b f", b=9)
    xbT = uf(OFF_XB, 4 * XBW, "p (c t) -> p c t", c=4)
    dT = ub(OFF_DT, 2 * TW, "p (c t) -> p c t", c=4)
    ptmp = uf(OFF_PT, 4 * 528, "p (r f) -> p r f", r=4)
    utmp = uf(OFF_UT, 1024, "p (r f) -> p r f", r=2)
    ttmp = uf(OFF_TT, 1024, "p (r f) -> p r f", r=2)
    yaT = uf(OFF_YA, 4 * TW, "p (c t) -> p c t", c=4)
    ybT = uf(OFF_YB, 4 * TW, "p (c t) -> p c t", c=4)
    actT = ub(OFF_ACT, 11 * TW, "p (c t) -> p c t", c=FC)
    stmp = uf(OFF_ST, 1536, "p (r f) -> p r f", r=3)
    NSTG = 8
    stg = uf(0, NSTG * 1024, "p (s f) -> p s f", s=NSTG)

    mm_banks = [ps("mm%d" % i) for i in range(4)]
    aux_banks = [ps("aux%d" % i) for i in range(4)]

    tokU = Buf("tokU")
    bx = [[Buf("x%d_%d" % (c, g)) for g in range(3)] for c in range(KC)]
    bh = [[Buf("h%d_%d" % (c, g)) for g in range(3)] for c in range(KC)]
    bxb = [[Buf("xb%d_%d" % (c, g), tokU) for g in range(3)] for c in range(4)]
    bxbh = Buf("xbh", tokU)
    bxbh2 = Buf("xbh2", tokU)
    bd = [[Buf("d%d_%d" % (c, g), tokU) for g in range(3)] for c in range(4)]
    bya = [[Buf("ya%d_%d" % (c, g), tokU) for g in range(3)] for c in range(4)]
    byb = [[Buf("yb%d_%d" % (c, g), tokU) for g in range(3)] for c in range(4)]
    bvn = [Buf("vn%d" % b, tokU) for b in range(9)]
    bact = [[Buf("a%d_%d" % (f, g), tokU) for g in range(3)] for f in range(FC)]
    MM = Ring([(mm_banks[i], Buf("mm%d" % i)) for i in range(4)])
    AUX = LiveRing([(aux_banks[i], Buf("aux%d" % i)) for i in range(4)])
    TR = AUX
    SQ = Ring([(i, Buf("sq%d" % i)) for i in range(4)])
    RS = Ring([(i, Buf("rs%d" % i)) for i in range(3)])
    STG = Ring([(i, Buf("stg%d" % i, tokU)) for i in range(8)])
    VT = Ring([(i, Buf("vt%d" % i, tokU)) for i in range(4)])
    UT = Ring([(i, Buf("ut%d" % i, tokU)) for i in range(2)])
    TTR = Ring([(i, Buf("tt%d" % i, tokU)) for i in range(2)])
    STM = Ring([(i, Buf("sm%d" % i, tokU)) for i in range(3)])
    SSV = Ring([(i, Buf("ssv%d" % i)) for i in range(2)])
    bpt = [Buf("pt%d" % i, tokU) for i in range(4)]
    btiny = Buf("tiny")
    bident = Buf("ident")
    binv = Buf("inv")
    bvec = Buf("vec")
    bgvb = Buf("gvb")
    bbias = Buf("bias")
    bwst = Buf("wst")
    bwpl = Buf("wpl")
    binvc = Buf("invc")
    bhalo = [Buf("halo%d" % l) for l in range(2)]
    bhalos = [Buf("halos%d" % l) for l in range(2)]
    bvout = Buf("vout")
    bhst = bvout
    bws = [Buf("ws%d" % i) for i in range(NSLOT)]
    bfence = Buf("fencecell")

    def gmix(l, c):
        return vecT[:, l * 8 + c:l * 8 + c + 1]

    def gffn(l, c):
        return vecT[:, 16 + l * 8 + c:16 + l * 8 + c + 1]

    def gbr(l, c):
        return vecT[:, 32 + l * 8 + c:32 + l * 8 + c + 1]

    def gfin(c):
        return vecT[:, 48 + c:48 + c + 1]

    def psc(l, g):
        return vecT[:, 56 + l * 4 + g:56 + l * 4 + g + 1]

    def mm_op(out, lhsT, rhs, start, stop, reads, writes):
        P.op("pe", lambda e: e.matmul(out, lhsT=lhsT, rhs=rhs, start=start, stop=stop), reads, writes)

    def tr_op(out, in_, idn, reads, writes):
        P.op("pe", lambda e: e.transpose(out, in_, idn), reads + [bident], writes)

    def act_op(out, in_, func, reads, writes, **kw):
        P.op("act", lambda e: e.activation(out=out, in_=in_, func=func, **kw), reads, writes)

    def dve_tt(out, in0, in1, op, reads, writes):
        P.op("dve", lambda e: e.tensor_tensor(out=out, in0=in0, in1=in1, op=op), reads, writes)

    def dve_stt(out, in0, scalar, in1, op0, op1, reads, writes):
        P.op("dve", lambda e: e.scalar_tensor_tensor(out=out, in0=in0, scalar=scalar, in1=in1,
                                                     op0=op0, op1=op1), reads, writes)

    def pool_tt(out, in0, in1, op, reads, writes):
        P.op("pool", lambda e: e.tensor_tensor(out=out, in0=in0, in1=in1, op=op), reads, writes)

    def pool_stt(out, in0, scalar, in1, op0, op1, reads, writes):
        P.op("pool", lambda e: e.scalar_tensor_tensor(out=out, in0=in0, scalar=scalar, in1=in1,
                                                      op0=op0, op1=op1), reads, writes)

    def dve_copy(out, in_, reads, writes):
        P.op("dve", lambda e: e.tensor_copy(out, in_), reads, writes)

    def dve_recip(out, in_, reads, writes):
        P.op("dve", lambda e: e.reciprocal(out, in_), reads, writes)

    def dve_memset(ap, val, writes):
        P.op("dve", lambda e: e.memset(ap, val), [], writes)

    def dma(eng, out, in_, reads, writes, lane, is_out=False):
        P.op(eng, lambda e: e.dma_start(out=out, in_=in_), reads, writes, lane=lane, is_out=is_out)

    def fence():
        P.op("dve", lambda e: e.memset(fence_cell[:, 0:1], 0.0), [], [tokU, bfence])

    wlist = []
    tiles = []
    for s_ in range(NSEQ):
        nt = SEQ // TT
        for ti in range(nt):
            t = Tile()
            first = ti == 0
            last = ti == nt - 1
            r0 = s_ * SEQ + ti * TT
            t.groups = [
                Group(0, 0, 512, "p", r0, XB_P, "zero" if first else "halo", first=first),
                Group(1, 512, 512, "p", r0 + 512, XB_P + 512, "prev", halo_save=not last,
                      hist_out=("p", s_) if last else None),
            ]
            tiles.append(t)
    if SAMPLE:
        tiles[-1].groups.append(Group(2, TT, NSW, "s", 0, XB_S, "halos", hist_out=("s",)))

    for t in tiles:
        for l in range(NLAYER):
            wlist.append(("win", l, 512, 512))
            wlist.append(("win", l, 1024, 512))
            wlist.append(("win", l, 0, 512))
            wlist.append(("wout", l, 0, 512))
            wlist.append(("wout", l, 512, 512))
            for j in range(6):
                nco = 512 if j < 5 else 256
                wlist.append(("gate", l, j * 512, nco))
                wlist.append(("up", l, j * 512, nco))
            for o in range(8):
                wlist.append(("down", l, o * 128, 128))
    wstate = {"next_load": 0, "next_use": 0}
    wsrc = {"win": w_in, "wout": w_out, "gate": w_gate, "up": w_up, "down": w_down}

    def w_view(slot, kind, nco):
        if kind == "down":
            return wsl[:, slot, 0:FC * 128].rearrange("p (k f) -> p k f", k=FC)
        return wsl[:, slot, 0:KC * nco].rearrange("p (k f) -> p k f", k=KC)

    def w_emit_load(i):
        kind, l, c0, nco = wlist[i]
        slot = i % NSLOT
        src = wsrc[kind][l].rearrange("(k p) f -> p k f", p=128)[:, :, c0:c0 + nco]
        dma("pool", w_view(slot, kind, nco), src, [], [bws[slot]], lane=("w", slot))

    wstate["released"] = 0

    def w_pump():
        while wstate["next_load"] < min(wstate["released"] + NSLOT, len(wlist)):
            w_emit_load(wstate["next_load"])
            wstate["next_load"] += 1

    def w_release(n):
        wstate["released"] += n
        assert wstate["released"] <= wstate["next_use"]
        w_pump()

    def w_acquire(kind, l, c0, hold=0):
        j = wstate["next_use"]
        assert wlist[j][:3] == (kind, l, c0), (wlist[j], kind, l, c0)
        w_pump()
        assert j < wstate["next_load"], "weight chunk not loaded: missing w_release"
        wstate["next_use"] += 1
        slot = j % NSLOT
        return w_view(slot, kind, wlist[j][3]), bws[slot]

    P.op("pool", lambda e: e.memset(ident[:], 0.0), [], [bident])
    P.op("pool", lambda e: e.affine_select(out=ident[:], in_=ident[:], pattern=[[-1, 128]],
                                           compare_op=ALU.not_equal, fill=1.0, base=0,
                                           channel_multiplier=1), [bident], [bident])
    dve_memset(inv1024[:], 1.0 / 1024.0, [binv])
    dve_memset(inv512[:], 1.0 / 512.0, [binv])
    dve_memset(fence_cell[:], 0.0, [bfence])
    for g, wd in enumerate(WINDOWS):
        dve_memset(invc[:, g, :], 1.0 / wd, [binvc])
        for tcol in range(wd - 1):
            dve_memset(invc[:, g, tcol:tcol + 1], 1.0 / (tcol + 1), [binvc])
    dve_memset(halo[:], 0.0, [bhalo[0], bhalo[1]])

    s0, bs0 = STG.next()
    vrows = stg[0:64, s0, 0:128]
    dma("sp", stg[0:16, s0, 0:128], g_mix.rearrange("l (c p) -> (l c) p", p=128), [], [bs0], ("c", 0))
    dma("sp", stg[16:32, s0, 0:128], g_ffn.rearrange("l (c p) -> (l c) p", p=128), [], [bs0], ("c", 1))
    dma("sp", stg[32:48, s0, 0:128], g_branch.rearrange("l (c p) -> (l c) p", p=128), [], [bs0], ("c", 2))
    dma("sp", stg[48:56, s0, 0:128], g_final.rearrange("l (c p) -> (l c) p", p=128), [], [bs0], ("c", 3))
    dma("sp", stg[56:64, s0, 0:128], pool_scale.rearrange("l (c p) -> (l c) p", p=128), [], [bs0], ("c", 0))
    tr, btr = TR.next()
    tr_op(tr[:, 0:64], vrows, ident[0:64, 0:64], [bs0], [btr])
    dve_copy(vecT[:, :], tr[:, 0:64], [btr], [bvec])
    for l in range(2):
        dma("sp", gvb[:, l, :], g_v[l:l + 1, :].partition_broadcast(128), [], [bgvb], ("c", 1))
        for c in range(4):
            for half in range(2):
                hh = 2 * c + half
                dma("sp", biasb[half * 64:(half + 1) * 64, l, c, :],
                    b_sp[l, hh:hh + 1, :].partition_broadcast(64), [], [bbias], ("c", 2 + half))
    for l in range(2):
        s1, bs1 = STG.next()
        wsn = stg[:, s1, :].rearrange("p (h j) -> p h j", h=8)
        dma("sp", wsn, w_sp[l].rearrange("h i j -> i h j"), [], [bs1], ("c", 0))
        dve_memset(wsn[0:64, :, 64:128], 0.0, [bs1])
        for half in range(2):
            tr, btr = TR.next()
            trv = tr[:, :].rearrange("p (j f) -> p j f", j=4)
            for j in range(4):
                tr_op(trv[:, j, :], wsn[:, half * 4 + j, :], ident[:, :], [bs1], [btr])
            dve_copy(WsT[:, l, half * 4:half * 4 + 4, :], trv, [btr], [bwst])
        dma("pool", wpl[:, l, :, :], w_pool[l].rearrange("g c d -> c g d"), [], [bwpl], ("c", 4))
    if SAMPLE:
        for l in range(2):
            s1, bs1 = STG.next()
            dma("sp", stg[0:HIST, s1, 0:512], sst[l], [], [bs1], ("c", 1))
            tr, btr = TR.next()
            trv = tr[:, :].rearrange("p (j f) -> p j f", j=4)
            for c in range(4):
                tr_op(trv[:, c, 0:HIST], stg[0:HIST, s1, c * 128:(c + 1) * 128], ident[0:HIST, 0:HIST],
                      [bs1], [btr])
            dve_copy(halos[:, l, :, 0:HIST], trv[:, :, 0:HIST], [btr], [bhalos[l]])
    for b in (bident, binv, bvec, bgvb, bbias, bwst, bwpl, binvc):
        b.const = True

    pend = []

    def flush():
        for f in pend:
            f()
        del pend[:]

    def cols(g):
        return slice(g.c0, g.c0 + g.w)

    def stats_sq(g, src_ap, src_buf, st, bst, first, last, invt, defer=True):
        w = g.w
        s, bs = SQ.next()
        act_op(sqr[:, s, 0:w], src_ap, AF.Square, [src_buf], [bs])

        def f():
            mm_op(st[:, 0:w], invt[:, :], sqr[:, s, 0:w], first, last, [bs, binv], [bst])
        if defer:
            pend.append(f)
        else:
            f()

    def rstd_finish(g, st, bst):
        w = g.w
        r, br = RS.next()
        act_op(rsd[:, r, 0:w], st[:, 0:w], AF.Ln, [bst], [br], bias=eps_ap[:, 0:1])
        act_op(rsd[:, r, 0:w], rsd[:, r, 0:w], AF.Exp, [br], [br], scale=-0.5)
        AUX.release(bst)
        return r, br

    def norm_apply(g, src_ap, src_buf, n, r, br, gcol, dst_ap, dst_buf):
        w = g.w
        for c in range(n):
            dve_stt(dst_ap(c), src_ap(c), gcol(c), rsd[:, r, 0:w], ALU.mult, ALU.mult,
                    [src_buf(c), br, bvec], [dst_buf(c)])

    def prep_norm(g, st, gcol, dst_ap, dst_buf):
        flush()
        r, br = rstd_finish(g, *st)
        norm_apply(g, lambda c: xT[:, c, cols(g)], lambda c: bx[c][g.gi], KC, r, br, gcol, dst_ap, dst_buf)

    def prep_norm1(l):
        return lambda g, st: prep_norm(g, st, lambda c: gmix(l, c), lambda c: hT[:, c, cols(g)],
                                       lambda c: bh[c][g.gi])

    def prep_ffn(l):
        return lambda g, st: prep_norm(g, st, lambda c: gffn(l, c), lambda c: hT[:, c, cols(g)],
                                       lambda c: bh[c][g.gi])

    def prep_final():
        return lambda g, st: prep_norm(g, st, lambda c: gfin(c), lambda c: xT[:, c, cols(g)],
                                       lambda c: bx[c][g.gi])

    def full_stats(g):
        st, bst = AUX.next(hold=True)
        for c in range(KC):
            stats_sq(g, xT[:, c, cols(g)], bx[c][g.gi], st, bst, c == 0, c == KC - 1, inv1024, defer=False)
        return st, bst

    def load_block(t, g, b):
        src = xp if g.kind == "p" else xs
        bw = g.bw
        r0 = g.row0 + b * 128
        s, bs = STG.next()
        dma("pool", stg[0:bw, s, :], src[r0:r0 + bw, :], [], [bs], ("ld", s))
        col = g.c0 + b * 128
        for half in range(2):
            tr, btr = AUX.next()
            trv = tr[:, :].rearrange("p (j f) -> p j f", j=4)
            for j in range(4):
                c = half * 4 + j
                tr_op(trv[:, j, 0:bw], stg[0:bw, s, c * 128:(c + 1) * 128], ident[0:bw, 0:bw],
                      [bs], [btr])
            dve_copy(xT[:, half * 4:half * 4 + 4, col:col + bw], trv[:, :, 0:bw],
                     [btr], [bx[half * 4 + j][g.gi] for j in range(4)])

    def load_tile(t):
        sts = {}
        for g in t.groups:
            for b in range(g.nblk):
                load_block(t, g, b)
        for g in t.groups:
            sts[g.gi] = full_stats(g)
        return sts

    def mixer(t, l, st_in, done):
        G = t.groups
        flush()
        for g in G:
            if g.gi not in done:
                prep_norm1(l)(g, st_in[g.gi])
        for g in G:
            hcols = slice(g.xb0 - HIST, g.xb0)
            if g.halo_src == "halos":
                dve_copy(xbT[:, :, hcols], halos[:, l, :, 0:HIST], [bhalos[l]], [bxbh2])
            elif g.halo_src == "zero":
                dve_memset(xbT[:, :, hcols], 0.0, [bxbh])
            elif g.halo_src == "halo":
                dve_copy(xbT[:, :, hcols], halo[:, l, :, 0:HIST], [bhalo[l]], [bxbh])
        wv, bwv = w_acquire("win", l, 512)
        wx, bwx = w_acquire("win", l, 1024, hold=1)
        for g in G:
            sv, bsv = SSV.next()
            bw = g.bw
            nb_ = g.nblk
            vts = []
            for b in range(nb_):
                col = g.c0 + b * 128
                mm, bmm = MM.next()
                for k in range(KC):
                    mm_op(mm[0:bw, :], hT[:, k, col:col + bw], wv[:, k, :], k == 0, k == KC - 1,
                          [bh[k][g.gi], bwv], [bmm])
                vt, bvt = VT.next()
                vts.append((vt, bvt))
                act_op(vtmp[0:bw, vt, :], mm[0:bw, :], AF.Gelu, [bmm], [bvt])
                s, bs = SQ.next()
                act_op(sqr[0:bw, s, :], vtmp[0:bw, vt, :], AF.Square, [bvt], [bs, bsv],
                       accum_out=ssv[0:bw, sv * 4 + b:sv * 4 + b + 1])
            P.op("dve", lambda e, sv=sv, bw=bw, nb_=nb_: e.tensor_scalar(
                out=rv[0:bw, sv * 4:sv * 4 + nb_], in0=ssv[0:bw, sv * 4:sv * 4 + nb_], scalar1=1.0 / DG,
                scalar2=EPS, op0=ALU.mult, op1=ALU.add), [bsv], [bsv])
            act_op(rv[0:bw, sv * 4:sv * 4 + nb_], rv[0:bw, sv * 4:sv * 4 + nb_], AF.Ln, [bsv], [bsv])
            act_op(rv2[0:bw, sv * 4:sv * 4 + nb_], rv[0:bw, sv * 4:sv * 4 + nb_], AF.Exp, [bsv], [bsv],
                   scale=-0.5)
            for b in range(nb_):
                vt, bvt = vts[b]
                bi = g.gi * 4 + b
                rcol = rv2[0:bw, sv * 4 + b:sv * 4 + b + 1]
                if g.kind == "s":
                    dve_stt(vout[0:bw, :], vtmp[0:bw, vt, :], rcol, gvb[0:bw, l, :],
                            ALU.mult, ALU.mult, [bvt, bsv, bgvb], [bvout])
                    dve_copy(vn[0:bw, bi, :], vout[0:bw, :], [bvout], [bvn[bi]])
                    dma("sp", vs[l, 0:bw, :], vout[0:bw, :], [bvout], [], ("vout", 0), is_out=True)
                else:
                    dve_stt(vn[0:bw, bi, :], vtmp[0:bw, vt, :], rcol, gvb[0:bw, l, :],
                            ALU.mult, ALU.mult, [bvt, bsv, bgvb], [bvn[bi]])
            for c in range(4):
                mm, bmm = MM.next()
                for k in range(KC):
                    mm_op(mm[:, 0:g.w], wx[:, k, c * 128:(c + 1) * 128], hT[:, k, cols(g)],
                          k == 0, k == KC - 1, [bh[k][g.gi], bwx], [bmm])
                act_op(xbT[:, c, g.xb0:g.xb0 + g.w], mm[:, 0:g.w], AF.Copy,
                       [bmm], [bxb[c][g.gi]])
        for gl in G:
            if gl.halo_save:
                dve_copy(halo[:, l, :, 0:HIST], xbT[:, :, gl.xb0 + gl.w - HIST:gl.xb0 + gl.w],
                         [bxb[c][gl.gi] for c in range(4)], [bhalo[l]])
        for gl in G:
            if gl.hist_out is None:
                continue
            tr, btr = AUX.next()
            trv = tr[:, :].rearrange("p (j f) -> p j f", j=4)
            for c in range(4):
                tr_op(trv[0:HIST, c, :], xbT[:, c, gl.xb0 + gl.w - HIST:gl.xb0 + gl.w], ident[:, :],
                      [bxb[c][gl.gi]], [btr])
            dve_copy(hst[0:HIST, :].rearrange("p (j f) -> p j f", j=4), trv[0:HIST, :, :], [btr], [bhst])
            dst = hp[l, gl.hist_out[1]] if gl.hist_out[0] == "p" else hs[l]
            dma("sp", dst, hst[0:HIST, :], [bhst], [], ("hst", 0), is_out=True)
        def pool_win(g, gi, eng):
            w = g.w
            wd = WINDOWS[gi]
            base = g.xb0
            if eng == "pool":
                tt_, A, B = pool_tt, 0, 1
            else:
                tt_, A, B = dve_tt, 2, 3

            def X(a, b):
                return xbT[:, gi, base + a:base + b]

            prevb = {"prev": bxb[gi][g.gi - 1], "halos": bxbh2}.get(g.halo_src, bxbh)
            rd = [bxb[gi][g.gi], prevb]
            ext = wd - 2
            tt_(ptmp[:, A, 0:w + ext], X(-ext, w), X(-ext - 1, w - 1), ALU.add, rd, [bpt[A]])
            cur, oth = A, B
            span = 2
            while span < wd:
                ext2 = wd - 2 * span
                tt_(ptmp[:, oth, 0:w + ext2], ptmp[:, cur, span:span + w + ext2],
                    ptmp[:, cur, 0:w + ext2], ALU.add, [bpt[cur]], [bpt[oth]])
                cur, oth = oth, cur
                span *= 2
            if g.first:
                tt_(tiny[:, 0:16], ptmp[:, cur, 0:16], invc[:, gi, :], ALU.mult,
                    [bpt[cur], binvc], [btiny])
            if eng == "pool":
                pool_tt(ptmp[:, oth, 0:w], ptmp[:, cur, 0:w], invc[:, gi, 15:16].to_broadcast([128, w]),
                        ALU.mult, [bpt[cur], binvc], [bpt[oth]])
                pool_tt(dT[:, gi, cols(g)], ptmp[:, oth, 0:w], X(0, w), ALU.subtract,
                        [bpt[oth], bxb[gi][g.gi]], [bd[gi][g.gi]])
            else:
                dve_stt(dT[:, gi, cols(g)], ptmp[:, cur, 0:w], 1.0 / wd, X(0, w), ALU.mult, ALU.subtract,
                        [bpt[cur], bxb[gi][g.gi]], [bd[gi][g.gi]])
            if g.first:
                tt_(dT[:, gi, g.c0:g.c0 + 16], tiny[:, 0:16], X(0, 16), ALU.subtract,
                    [btiny, bxb[gi][g.gi]], [bd[gi][g.gi]])

        w_release(2)
        for g in G:
            for gi in (3, 2):
                pool_win(g, gi, "pool")
        wu, bwu = w_acquire("win", l, 0)
        sts_a = {}
        sts_b = {}
        pool_q = []

        def pool_mm(g, gi, stb, bstb):
            w = g.w
            mm, bmm = MM.next()
            mm_op(mm[:, 0:w], wpl[:, l, gi, :], dT[:, gi, cols(g)], True, True,
                  [bd[gi][g.gi], bwpl], [bmm])
            flush()
            act_op(ybT[:, gi, cols(g)], mm[:, 0:w], AF.Copy, [bmm, bvec], [byb[gi][g.gi]],
                   scale=psc(l, gi))
            if not g.small:
                stats_sq(g, ybT[:, gi, cols(g)], byb[gi][g.gi], stb, bstb, gi == 0, gi == 3, inv512)

        def pool_drain(g=None, n=None):
            k = 0
            while pool_q and (g is None or pool_q[0][0] is g) and (n is None or k < n):
                pool_q.pop(0)[1]()
                k += 1

        for g in G:
            w = g.w
            bw = g.bw
            if g.small:
                sta = bsta = stb = bstb = None
            else:
                sta, bsta = AUX.next(hold=True)
                stb, bstb = AUX.next(hold=True)
                sts_a[g.gi] = (sta, bsta)
                sts_b[g.gi] = (stb, bstb)
            for c in range(4):
                mmx, bmx = MM.next()
                for half in range(2):
                    hh = 2 * c + half
                    for b in range(g.nblk):
                        bi = g.gi * 4 + b
                        mm_op(mmx[half * 64:(half + 1) * 64, b * 128:b * 128 + bw],
                              vn[0:bw, bi, hh * 64:(hh + 1) * 64], WsT[0:bw, l, hh, 0:bw], True, True,
                              [bvn[bi], bwst], [bmx])
                mmy, bmy = MM.next()
                for k in range(KC):
                    mm_op(mmy[:, 0:w], wu[:, k, c * 128:(c + 1) * 128], hT[:, k, cols(g)],
                          k == 0, k == KC - 1, [bh[k][g.gi], bwu], [bmy])
                flush()
                pool_drain(n=1)
                ut, but = UT.next()
                act_op(utmp[:, ut, 0:w], mmy[:, 0:w], AF.Gelu, [bmy], [but])
                tt, btt = TTR.next()
                if g.nblk > 1:
                    dve_tt(ttmp[:, tt, 0:w].rearrange("p (b i) -> p b i", b=g.nblk),
                           mmx[:, 0:w].rearrange("p (b i) -> p b i", b=g.nblk),
                           biasb[:, l, c, :].unsqueeze(1).to_broadcast([128, g.nblk, 128]),
                           ALU.add, [bmx, bbias], [btt])
                else:
                    dve_tt(ttmp[:, tt, 0:w], mmx[:, 0:w], biasb[:, l, c, 0:w], ALU.add, [bmx, bbias], [btt])
                dve_tt(yaT[:, c, cols(g)], ttmp[:, tt, 0:w], utmp[:, ut, 0:w], ALU.mult,
                       [btt, but], [bya[c][g.gi]])
                if not g.small:
                    stats_sq(g, yaT[:, c, cols(g)], bya[c][g.gi], sta, bsta, c == 0, c == 3, inv512)
                if c < 2:
                    pool_win(g, c, "dve")
            for gi in range(4):
                pool_q.append((g, lambda g=g, gi=gi, stb=stb, bstb=bstb: pool_mm(g, gi, stb, bstb)))
        w_release(1)
        wo = [None, None]
        bwo = [None, None]
        wo[0], bwo[0] = w_acquire("wout", l, 0)
        wo[1], bwo[1] = w_acquire("wout", l, 512, hold=1)
        st_out = {}
        done_ffn = set()

        def prep_wout(g):
            pool_drain(g=g)
            flush()
            if g.small:
                sta, bsta = AUX.next(hold=True)
                for c in range(4):
                    stats_sq(g, yaT[:, c, cols(g)], bya[c][g.gi], sta, bsta, c == 0, c == 3, inv512, defer=False)
                ra, bra = rstd_finish(g, sta, bsta)
                stb, bstb = AUX.next(hold=True)
                for c in range(4):
                    stats_sq(g, ybT[:, c, cols(g)], byb[c][g.gi], stb, bstb, c == 0, c == 3, inv512, defer=False)
                rb, brb = rstd_finish(g, stb, bstb)
            else:
                ra, bra = rstd_finish(g, *sts_a[g.gi])
                rb, brb = rstd_finish(g, *sts_b[g.gi])
            norm_apply(g, lambda c: yaT[:, c, cols(g)], lambda c: bya[c][g.gi], 4, ra, bra,
                       lambda c: gbr(l, c), lambda c: hT[:, c, cols(g)], lambda c: bh[c][g.gi])
            norm_apply(g, lambda c: ybT[:, c, cols(g)], lambda c: byb[c][g.gi], 4, rb, brb,
                       lambda c: gbr(l, 4 + c), lambda c: hT[:, 4 + c, cols(g)], lambda c: bh[4 + c][g.gi])

        prep_wout(G[0])
        for k, g in enumerate(G):
            st, bst = AUX.next(hold=True)
            st_out[g.gi] = (st, bst)
            u = 0
            for ch in range(2):
                for oc in range(4):
                    o = ch * 4 + oc
                    mm, bmm = MM.next()
                    for kk in range(KC):
                        mm_op(mm[:, 0:g.w], wo[ch][:, kk, oc * 128:(oc + 1) * 128], hT[:, kk, cols(g)],
                              kk == 0, kk == KC - 1, [bh[kk][g.gi], bwo[ch]], [bmm])
                    flush()
                    pool_drain(n=1)
                    dve_tt(xT[:, o, cols(g)], xT[:, o, cols(g)], mm[:, 0:g.w], ALU.add,
                           [bx[o][g.gi], bmm], [bx[o][g.gi]])
                    stats_sq(g, xT[:, o, cols(g)], bx[o][g.gi], st, bst, o == 0, o == 7, inv1024)
                    u += 1
                    if u == 2 and k >= 1:
                        pg = G[k - 1]
                        prep_ffn(l)(pg, st_out[pg.gi])
                        done_ffn.add(pg.gi)
                    if u == 5 and k + 1 < len(G):
                        prep_wout(G[k + 1])
        w_release(2)
        return st_out, done_ffn

    def ffn(t, l, st_in, done, next_prep):
        G = t.groups
        for g in G:
            if g.gi not in done:
                prep_ffn(l)(g, st_in[g.gi])
        for j in range(6):
            nco = 512 if j < 5 else 256
            wg, bwg = w_acquire("gate", l, j * 512)
            wu, bwu = w_acquire("up", l, j * 512, hold=1)
            for g in G:
                w = g.w
                for fc in range(nco // 128):
                    f = j * 4 + fc
                    ma, bma = MM.next()
                    for k in range(KC):
                        mm_op(ma[:, 0:w], wg[:, k, fc * 128:(fc + 1) * 128], hT[:, k, cols(g)],
                              k == 0, k == KC - 1, [bh[k][g.gi], bwg], [bma])
                    mb, bmb = MM.next()
                    for k in range(KC):
                        mm_op(mb[:, 0:w], wu[:, k, fc * 128:(fc + 1) * 128], hT[:, k, cols(g)],
                              k == 0, k == KC - 1, [bh[k][g.gi], bwu], [bmb])
                    sm, bsm = STM.next()
                    act_op(stmp[:, sm, 0:w], ma[:, 0:w], AF.Silu, [bma], [bsm])
                    dve_tt(actT[:, f, cols(g)], stmp[:, sm, 0:w], mb[:, 0:w], ALU.mult,
                           [bsm, bmb], [bact[f][g.gi]])
            w_release(2)
        st_out = {}
        done_next = set()
        for g in G:
            st_out[g.gi] = AUX.next(hold=True)
        for op_ in range(4):
            wd_ = [None, None]
            bwd = [None, None]
            wd_[0], bwd[0] = w_acquire("down", l, (2 * op_) * 128)
            wd_[1], bwd[1] = w_acquire("down", l, (2 * op_ + 1) * 128, hold=1)
            for kg, g in enumerate(G):
                st, bst = st_out[g.gi]
                for i2 in range(2):
                    o = 2 * op_ + i2
                    mm, bmm = MM.next()
                    for k in range(FC):
                        mm_op(mm[:, 0:g.w], wd_[i2][:, k, :], actT[:, k, cols(g)], k == 0, k == FC - 1,
                              [bact[k][g.gi], bwd[i2]], [bmm])
                    flush()
                    dve_tt(xT[:, o, cols(g)], xT[:, o, cols(g)], mm[:, 0:g.w], ALU.add,
                           [bx[o][g.gi], bmm], [bx[o][g.gi]])
                    stats_sq(g, xT[:, o, cols(g)], bx[o][g.gi], st, bst, o == 0, o == 7, inv1024)
                    if op_ == 3 and i2 == 1 and kg >= 1:
                        pg = G[kg - 1]
                        next_prep(pg, st_out[pg.gi])
                        done_next.add(pg.gi)
            w_release(2)
        return st_out, done_next

    def final_block(t, g, b):
        dst = yp if g.kind == "p" else ys
        bw = g.bw
        col = g.c0 + b * 128
        r0 = g.row0 + b * 128
        s, bs = STG.next()
        for half in range(2):
            tr, btr = AUX.next()
            trv = tr[:, :].rearrange("p (j f) -> p j f", j=4)
            for j in range(4):
                c = half * 4 + j
                tr_op(trv[0:bw, j, :], xT[:, c, col:col + bw], ident[:, :], [bx[c][g.gi]], [btr])
            act_op(stg[0:bw, s, half * 512:(half + 1) * 512].rearrange("p (j f) -> p j f", j=4),
                   trv[0:bw, :, :], AF.Copy, [btr], [bs])
        dma("sp", dst[r0:r0 + bw, :], stg[0:bw, s, :], [bs], [], ("stg", s), is_out=True)

    def final_and_load(t, st_in, nxt, done):
        for g in t.groups:
            if g.gi not in done:
                prep_final()(g, st_in[g.gi])
        ngroups = {g.gi: g for g in (nxt.groups if nxt is not None else [])}
        pendL = []
        for g in t.groups:
            for b in range(g.nblk):
                final_block(t, g, b)
                if pendL:
                    ng, nb = pendL.pop(0)
                    load_block(nxt, ng, nb)
            if g.gi in ngroups:
                ng = ngroups.pop(g.gi)
                pendL += [(ng, nb) for nb in range(ng.nblk)]
        for ng, nb in pendL:
            load_block(nxt, ng, nb)
        for ng in ngroups.values():
            for nb in range(ng.nblk):
                load_block(nxt, ng, nb)
        sts = {}
        if nxt is not None:
            for g in nxt.groups:
                sts[g.gi] = full_stats(g)
        return sts

    eps_ap = sb("eps_ap", [128, 1], F32)
    beps = Buf("eps")
    dve_memset(eps_ap[:], EPS, [beps])
    beps.const = True
    P.op("act", lambda e: e.activation(out=fence_cell[:, 1:2], in_=eps_ap[:, 0:1], func=AF.Copy),
         [beps], [bfence])

    st = load_tile(tiles[0])
    for ti, t in enumerate(tiles):
        done = set()
        for l in range(NLAYER):
            fence()
            st, done = mixer(t, l, st, done)
            if DEBUG and t is tiles[0] and l == 0:
                dma("sp", dbg_x1, xT[:, :, :], [b2 for row in bx for b2 in row], [], ("dbg", 4), is_out=True)
            fence()
            nprep = prep_norm1(l + 1) if l + 1 < NLAYER else prep_final()
            st, done = ffn(t, l, st, done, nprep)
        fence()
        st = final_and_load(t, st, tiles[ti + 1] if ti + 1 < len(tiles) else None, done)
    flush()

    assert wstate["next_use"] == len(wlist)
    P.finalize(nc)
    return nc, P


_CACHE = {}


def kernel(x_prompt, x_sample, state_pool, g_mix, w_in, g_v, w_spatial, b_spatial, w_pool, pool_scale,
           g_branch, w_out, g_ffn, w_gate, w_up, w_down, g_final):
    f = lambda a: np.ascontiguousarray(np.asarray(a, dtype=np.float32))
    x_prompt = f(x_prompt)
    x_sample = f(x_sample)
    state_pool = f(state_pool)
    B, S, _ = x_prompt.shape
    nseq = B // N_CORES
    if "nc" not in _CACHE:
        _CACHE["nc"] = build_program(NSEQ=nseq, SEQ=S, SAMPLE=True)[0]
    nc = _CACHE["nc"]
    shared = {
        "g_mix": f(g_mix), "w_in": f(w_in), "g_v": f(g_v), "w_sp": f(w_spatial), "b_sp": f(b_spatial),
        "w_pool": f(w_pool), "pool_scale": f(pool_scale), "g_branch": f(g_branch), "w_out": f(w_out),
        "g_ffn": f(g_ffn), "w_gate": f(w_gate), "w_up": f(w_up), "w_down": f(w_down),
        "g_final": f(g_final).reshape(1, D),
    }
    in_maps = []
    for i in range(N_CORES):
        m = dict(shared)
        m["xp"] = x_prompt[i * nseq:(i + 1) * nseq].reshape(nseq * S, D)
        m["xs"] = x_sample[i]
        m["sst"] = np.ascontiguousarray(state_pool[:, i])
        in_maps.append(m)
    res = run_bass_kernel_spmd(nc, in_maps, core_ids=list(range(N_CORES)))
    rs = res.results
    y_prompt = np.concatenate([r["yp"].reshape(nseq, S, D) for r in rs], axis=0)
    y_sample = np.stack([r["ys"] for r in rs], axis=0)
    hp = np.concatenate([r["hp"] for r in rs], axis=1)
    hs = np.stack([r["hs"] for r in rs], axis=1)
    vs = np.stack([r["vs"] for r in rs], axis=1)
    return (y_prompt.astype(np.float32), y_sample.astype(np.float32), hp.astype(np.float32),
            hs.astype(np.float32), vs.astype(np.float32))
```

```python
import numpy as np
import concourse.bass as bass
import concourse.mybir as mybir
from concourse.bass_utils import run_bass_kernel_spmd

F32 = mybir.dt.float32
BF16 = mybir.dt.bfloat16
AF = mybir.ActivationFunctionType
ALU = mybir.AluOpType

EPS = 1e-6
D = 1024
DG = 512
DFF = 2816
KC = 8
FC = 22
HIST = 15
WINDOWS = (2, 4, 8, 16)
N_CORES = 8


class Buf:
    __slots__ = ("name", "w", "rs", "const", "tok")

    def __init__(self, name, tok=None):
        self.name = name
        self.w = None
        self.rs = {}
        self.const = False
        self.tok = tok


class Op:
    __slots__ = ("eng", "fn", "deps", "lane", "signal", "sig", "clock", "waits", "idx")


class Prog:
    ENGS = ("pe", "act", "dve", "pool", "sp")

    def __init__(self):
        self.ops = []
        self.lane_last = {}
        self.out_ops = []

    def op(self, eng, fn, reads=(), writes=(), lane=None, is_out=False):
        o = Op()
        o.eng = eng
        o.fn = fn
        o.lane = lane
        o.signal = lane is not None
        o.idx = len(self.ops)
        o.sig = None
        deps = {}
        reads = list(reads)
        for b in list(reads) + list(writes):
            if b.tok is not None and b.tok not in reads:
                reads.append(b.tok)

        def add(d):
            if d is None:
                return
            if d.lane is None and d.eng == "pe" and eng == "pe" and lane is None:
                return
            key = d.eng if d.lane is None else ("lane", d.lane)
            cur = deps.get(key)
            if cur is None or cur.idx < d.idx:
                deps[key] = d

        for b in reads:
            add(b.w)
        for b in writes:
            add(b.w)
            for r in b.rs.values():
                add(r)
        if lane is not None:
            add(self.lane_last.get(lane))
            self.lane_last[lane] = o
        mykey = eng if lane is None else ("lane", lane)
        for b in reads:
            if not b.const:
                b.rs[mykey] = o
        for b in writes:
            b.w = o
            b.rs = {}
        o.deps = sorted(deps.values(), key=lambda d: d.idx)
        for d in o.deps:
            d.signal = True
        self.ops.append(o)
        if is_out:
            self.out_ops.append(o)
        return o

    def finalize(self, nc):
        fin = Op()
        fin.eng = "sp"
        fin.fn = None
        fin.lane = None
        fin.signal = False
        fin.idx = len(self.ops)
        fin.sig = None
        dd = {}
        for d in self.out_ops:
            key = ("lane", d.lane)
            if key not in dd or dd[key].idx < d.idx:
                dd[key] = d
        fin.deps = sorted(dd.values(), key=lambda d: d.idx)
        self.ops.append(fin)

        eng_count = {e: 0 for e in self.ENGS}
        lane_count = {}
        known = {e: {} for e in self.ENGS}
        nwaits = 0
        for o in self.ops:
            kn = known[o.eng]
            waits = {}
            for d in o.deps:
                key, val = d.sig
                if kn.get(key, 0) >= val:
                    continue
                if waits.get(key, 0) < val:
                    waits[key] = val
                for k2, v2 in d.clock.items():
                    if kn.get(k2, 0) < v2:
                        kn[k2] = v2
            o.waits = list(waits.items())
            nwaits += len(o.waits)
            if o.lane is not None:
                lane_count[o.lane] = lane_count.get(o.lane, 0) + 16
                o.sig = (("lane", o.lane), lane_count[o.lane])
            elif o.signal:
                eng_count[o.eng] += 1
                o.sig = (o.eng, eng_count[o.eng])
            if o.sig is not None:
                o.clock = dict(kn)
                o.clock[o.sig[0]] = o.sig[1]
            else:
                o.clock = None
        self.stats = dict(n_ops=len(self.ops), n_waits=nwaits, eng_count=eng_count,
                          n_lanes=len(lane_count))

        sems = {}
        for e in self.ENGS:
            sems[e] = nc.alloc_semaphore("s_" + e)
        for i, ln in enumerate(lane_count):
            sems[("lane", ln)] = nc.alloc_semaphore("l_%d" % i)

        per_eng = {e: [] for e in self.ENGS}
        for o in self.ops:
            per_eng[o.eng].append(o)

        def emit(ename, e):
            for o in per_eng[ename]:
                for key, val in o.waits:
                    e.wait_ge(sems[key], val)
                if o.fn is None:
                    continue
                ins = o.fn(e)
                if o.sig is not None:
                    ins.then_inc(sems[o.sig[0]], 16 if o.lane is not None else 1)

        with nc.Block() as block:
            @block.tensor
            def _(e):
                emit("pe", e)

            @block.scalar
            def _(e):
                emit("act", e)

            @block.vector
            def _(e):
                emit("dve", e)

            @block.gpsimd
            def _(e):
                emit("pool", e)

            @block.sync
            def _(e):
                emit("sp", e)


class Ring:
    def __init__(self, items):
        self.items = items
        self.i = 0

    def next(self):
        it = self.items[self.i % len(self.items)]
        self.i += 1
        return it


class LiveRing:
    def __init__(self, items):
        self.items = items
        self.live = [False] * len(items)
        self.i = 0

    def next(self, hold=False):
        for _ in range(len(self.items)):
            k = self.i % len(self.items)
            self.i += 1
            if not self.live[k]:
                self.live[k] = hold
                return self.items[k]
        raise RuntimeError("LiveRing exhausted")

    def release(self, buf):
        for k, it in enumerate(self.items):
            if it[1] is buf:
                self.live[k] = False
                return
        raise KeyError(buf.name)


class Group:
    def __init__(self, gi, c0, w, kind, row0, xb0, halo_src, first=False, halo_save=False, hist_out=None):
        self.gi = gi
        self.c0 = c0
        self.w = w
        self.nblk = (w + 127) // 128
        self.bw = min(128, w)
        self.kind = kind
        self.row0 = row0
        self.xb0 = xb0
        self.halo_src = halo_src
        self.first = first
        self.halo_save = halo_save
        self.hist_out = hist_out
        self.small = w < 512


class Tile:
    pass


def build_program(NSEQ=2, SEQ=4096, SAMPLE=True, NLAYER=2, DEBUG=False):
    nc = bass.Bass("TRN2", target_bir_lowering=False)
    P = Prog()
    TT = 1024
    NSW = 64
    TW = TT + NSW

    def din(name, shape):
        return nc.dram_tensor(name, list(shape), F32, kind="ExternalInput").ap()

    def dout(name, shape):
        return nc.dram_tensor(name, list(shape), F32, kind="ExternalOutput").ap()

    xp = din("xp", [NSEQ * SEQ, D])
    xs = din("xs", [NSW, D])
    sst = din("sst", [2, HIST, DG])
    g_mix = din("g_mix", [2, D])
    w_in = din("w_in", [2, D, 1536])
    g_v = din("g_v", [2, DG])
    w_sp = din("w_sp", [2, 8, 128, 128])
    b_sp = din("b_sp", [2, 8, 128])
    w_pool = din("w_pool", [2, 4, 128, 128])
    pool_scale = din("pool_scale", [2, DG])
    g_branch = din("g_branch", [2, D])
    w_out = din("w_out", [2, D, D])
    g_ffn = din("g_ffn", [2, D])
    w_gate = din("w_gate", [2, D, DFF])
    w_up = din("w_up", [2, D, DFF])
    w_down = din("w_down", [2, DFF, D])
    g_final = din("g_final", [1, D])

    yp = dout("yp", [NSEQ * SEQ, D])
    ys = dout("ys", [NSW, D])
    hp = dout("hp", [2, NSEQ, HIST, DG])
    hs = dout("hs", [2, HIST, DG])
    vs = dout("vs", [2, NSW, DG])
    if DEBUG:
        dbg_ya = dout("dbg_ya", [128, 4, 1024])
        dbg_yb = dout("dbg_yb", [128, 4, 1024])
        dbg_ym = nc.dram_tensor("dbg_ym", [128, 8, 1024], BF16, kind="ExternalOutput").ap()
        dbg_x1 = dout("dbg_x1", [128, 8, 1024])
        dbg_x2 = dout("dbg_x2", [128, 8, 1024])
        dbg_d = nc.dram_tensor("dbg_d", [128, 4, 1024], BF16, kind="ExternalOutput").ap()
        dbg_act = nc.dram_tensor("dbg_act", [128, 22, 1024], BF16, kind="ExternalOutput").ap()
        dbg_h2 = nc.dram_tensor("dbg_h2", [128, 8, 1024], BF16, kind="ExternalOutput").ap()

    def sb(name, shape, dt):
        return nc.alloc_sbuf_tensor(name, list(shape), dt)

    def ps(name):
        return nc.alloc_psum_tensor(name, [128, 512], F32)

    xT = sb("xT", [128, KC, TW], F32)
    hT = sb("hT", [128, KC, TW], BF16)
    sqr = sb("sqr", [128, 4, 512], BF16)
    rsd = sb("rsd", [128, 3, 512], F32)
    NSLOT = 4
    wsl = sb("wsl", [128, NSLOT, 4096], BF16)
    ident = sb("ident", [128, 128], F32)
    inv1024 = sb("inv1024", [128, 128], BF16)
    inv512 = sb("inv512", [128, 128], BF16)
    vecT = sb("vecT", [128, 64], F32)
    gvb = sb("gvb", [128, 2, 512], F32)
    biasb = sb("biasb", [128, 2, 4, 128], F32)
    WsT = sb("WsT", [128, 2, 8, 128], BF16)
    wpl = sb("wpl", [128, 2, 4, 128], BF16)
    invc = sb("invc", [128, 4, 16], F32)
    halo = sb("halo", [128, 2, 4, 16], F32)
    halos = sb("halos", [128, 2, 4, 16], F32)
    ssv = sb("ssv", [128, 8], F32)
    rv = sb("rv", [128, 8], F32)
    rv2 = sb("rv2", [128, 8], F32)
    tiny = sb("tiny", [128, 16], F32)
    hst = sb("hst", [16, 512], F32)
    vout = sb("vout", [128, 512], F32)
    fence_cell = sb("fence_cell", [128, 2], F32)

    XBW = 1120
    XB_P = HIST
    XB_S = HIST + TT + HIST
    OFF_VTMP = 0
    OFF_VN = OFF_VTMP + 2048
    OFF_XB = OFF_VN + 2304
    OFF_DT = OFF_XB + 4 * XBW
    OFF_PT = OFF_DT + 2 * TW
    OFF_UT = OFF_PT + 3 * 528
    OFF_TT = OFF_UT + 1024
    OFF_YA = OFF_TT + 1024
    OFF_YB = OFF_YA + 4 * TW
    U_END = OFF_YB + 4 * TW
    OFF_ACT = 0
    OFF_ST = 11 * TW
    assert OFF_ST + 1536 <= U_END
    U = sb("U", [128, U_END], F32)

    def uf(off, n, pat=None, **kw):
        v = U[:, off:off + n]
        if pat:
            v = v.rearrange(pat, **kw)
        return v

    def ub(off, nwords, pat=None, **kw):
        v = U[:, off:off + nwords].bitcast(BF16)
        if pat:
            v = v.rearrange(pat, **kw)
        return v

    vtmp = uf(OFF_VTMP, 2048, "p (r f) -> p r f", r=4)
    vn = ub(OFF_VN, 2304, "p (b f) -> p b f", b=9)
    xbT = uf(OFF_XB, 4 * XBW, "p (c t) -> p c t", c=4)
    dT = ub(OFF_DT, 2 * TW, "p (c t) -> p c t", c=4)
    ptmp = uf(OFF_PT, 3 * 528, "p (r f) -> p r f", r=3)
    utmp = uf(OFF_UT, 1024, "p (r f) -> p r f", r=2)
    ttmp = uf(OFF_TT, 1024, "p (r f) -> p r f", r=2)
    yaT = uf(OFF_YA, 4 * TW, "p (c t) -> p c t", c=4)
    ybT = uf(OFF_YB, 4 * TW, "p (c t) -> p c t", c=4)
    actT = ub(OFF_ACT, 11 * TW, "p (c t) -> p c t", c=FC)
    stmp = uf(OFF_ST, 1536, "p (r f) -> p r f", r=3)
    NSTG = 8
    OFF_STG = OFF_ST + 1536
    assert OFF_STG + NSTG * 1024 <= U_END
    stg = uf(OFF_STG, NSTG * 1024, "p (s f) -> p s f", s=NSTG)

    mm_banks = [ps("mm%d" % i) for i in range(4)]
    aux_banks = [ps("aux%d" % i) for i in range(4)]

    tokU = Buf("tokU")
    bx = [[Buf("x%d_%d" % (c, g)) for g in range(3)] for c in range(KC)]
    bh = [[Buf("h%d_%d" % (c, g)) for g in range(3)] for c in range(KC)]
    bxb = [[Buf("xb%d_%d" % (c, g), tokU) for g in range(3)] for c in range(4)]
    bxbh = Buf("xbh", tokU)
    bxbh2 = Buf("xbh2", tokU)
    bd = [[Buf("d%d_%d" % (c, g), tokU) for g in range(3)] for c in range(4)]
    bya = [[Buf("ya%d_%d" % (c, g), tokU) for g in range(3)] for c in range(4)]
    byb = [[Buf("yb%d_%d" % (c, g), tokU) for g in range(3)] for c in range(4)]
    bvn = [Buf("vn%d" % b, tokU) for b in range(9)]
    bact = [[Buf("a%d_%d" % (f, g), tokU) for g in range(3)] for f in range(FC)]
    MM = Ring([(mm_banks[i], Buf("mm%d" % i)) for i in range(4)])
    AUX = LiveRing([(aux_banks[i], Buf("aux%d" % i)) for i in range(4)])
    TR = AUX
    SQ = Ring([(i, Buf("sq%d" % i)) for i in range(4)])
    RS = Ring([(i, Buf("rs%d" % i)) for i in range(3)])
    STG = LiveRing([(i, Buf("stg%d" % i, tokU)) for i in range(8)])
    VT = Ring([(i, Buf("vt%d" % i, tokU)) for i in range(4)])
    UT = Ring([(i, Buf("ut%d" % i, tokU)) for i in range(2)])
    TTR = Ring([(i, Buf("tt%d" % i, tokU)) for i in range(2)])
    STM = Ring([(i, Buf("sm%d" % i, tokU)) for i in range(3)])
    SSV = Ring([(i, Buf("ssv%d" % i)) for i in range(2)])
    bpt = [Buf("pt%d" % i, tokU) for i in range(3)]
    btiny = Buf("tiny")
    bident = Buf("ident")
    binv = Buf("inv")
    bvec = Buf("vec")
    bgvb = Buf("gvb")
    bbias = Buf("bias")
    bwst = Buf("wst")
    bwpl = Buf("wpl")
    binvc = Buf("invc")
    bhalo = [Buf("halo%d" % l) for l in range(2)]
    bhalos = [Buf("halos%d" % l) for l in range(2)]
    bhst = Buf("hst")
    bvout = Buf("vout")
    bws = [Buf("ws%d" % i) for i in range(NSLOT)]
    bfence = Buf("fencecell")

    def gmix(l, c):
        return vecT[:, l * 8 + c:l * 8 + c + 1]

    def gffn(l, c):
        return vecT[:, 16 + l * 8 + c:16 + l * 8 + c + 1]

    def gbr(l, c):
        return vecT[:, 32 + l * 8 + c:32 + l * 8 + c + 1]

    def gfin(c):
        return vecT[:, 48 + c:48 + c + 1]

    def psc(l, g):
        return vecT[:, 56 + l * 4 + g:56 + l * 4 + g + 1]

    def mm_op(out, lhsT, rhs, start, stop, reads, writes):
        P.op("pe", lambda e: e.matmul(out, lhsT=lhsT, rhs=rhs, start=start, stop=stop), reads, writes)

    def tr_op(out, in_, idn, reads, writes):
        P.op("pe", lambda e: e.transpose(out, in_, idn), reads + [bident], writes)

    def act_op(out, in_, func, reads, writes, **kw):
        P.op("act", lambda e: e.activation(out=out, in_=in_, func=func, **kw), reads, writes)

    def dve_tt(out, in0, in1, op, reads, writes):
        P.op("dve", lambda e: e.tensor_tensor(out=out, in0=in0, in1=in1, op=op), reads, writes)

    def dve_stt(out, in0, scalar, in1, op0, op1, reads, writes):
        P.op("dve", lambda e: e.scalar_tensor_tensor(out=out, in0=in0, scalar=scalar, in1=in1,
                                                     op0=op0, op1=op1), reads, writes)

    def pool_tt(out, in0, in1, op, reads, writes):
        P.op("pool", lambda e: e.tensor_tensor(out=out, in0=in0, in1=in1, op=op), reads, writes)

    def pool_stt(out, in0, scalar, in1, op0, op1, reads, writes):
        P.op("pool", lambda e: e.scalar_tensor_tensor(out=out, in0=in0, scalar=scalar, in1=in1,
                                                      op0=op0, op1=op1), reads, writes)

    def dve_copy(out, in_, reads, writes):
        P.op("dve", lambda e: e.tensor_copy(out, in_), reads, writes)

    def dve_recip(out, in_, reads, writes):
        P.op("dve", lambda e: e.reciprocal(out, in_), reads, writes)

    def dve_memset(ap, val, writes):
        P.op("dve", lambda e: e.memset(ap, val), [], writes)

    def dma(eng, out, in_, reads, writes, lane, is_out=False):
        P.op(eng, lambda e: e.dma_start(out=out, in_=in_), reads, writes, lane=lane, is_out=is_out)

    def fence():
        P.op("dve", lambda e: e.memset(fence_cell[:, 0:1], 0.0), [], [tokU, bfence])

    wlist = []
    tiles = []
    for s_ in range(NSEQ):
        nt = SEQ // TT
        for ti in range(nt):
            t = Tile()
            first = ti == 0
            last = ti == nt - 1
            r0 = s_ * SEQ + ti * TT
            t.groups = [
                Group(0, 0, 512, "p", r0, XB_P, "zero" if first else "halo", first=first),
                Group(1, 512, 512, "p", r0 + 512, XB_P + 512, "prev", halo_save=not last,
                      hist_out=("p", s_) if last else None),
            ]
            tiles.append(t)
    if SAMPLE:
        tiles[-1].groups.append(Group(2, TT, NSW, "s", 0, XB_S, "halos", hist_out=("s",)))

    for t in tiles:
        for l in range(NLAYER):
            wlist.append(("win", l, 512, 512))
            wlist.append(("win", l, 1024, 512))
            wlist.append(("win", l, 0, 512))
            wlist.append(("wout", l, 0, 512))
            wlist.append(("wout", l, 512, 512))
            for j in range(6):
                nco = 512 if j < 5 else 256
                wlist.append(("gate", l, j * 512, nco))
                wlist.append(("up", l, j * 512, nco))
            for o in range(8):
                wlist.append(("down", l, o * 128, 128))
    wstate = {"next_load": 0, "next_use": 0}
    wsrc = {"win": w_in, "wout": w_out, "gate": w_gate, "up": w_up, "down": w_down}

    def w_view(slot, kind, nco):
        if kind == "down":
            return wsl[:, slot, 0:FC * 128].rearrange("p (k f) -> p k f", k=FC)
        return wsl[:, slot, 0:KC * nco].rearrange("p (k f) -> p k f", k=KC)

    def w_emit_load(i):
        kind, l, c0, nco = wlist[i]
        slot = i % NSLOT
        src = wsrc[kind][l].rearrange("(k p) f -> p k f", p=128)[:, :, c0:c0 + nco]
        dma("pool", w_view(slot, kind, nco), src, [], [bws[slot]], lane=("w", slot))

    def w_acquire(kind, l, c0, hold=0):
        j = wstate["next_use"]
        assert wlist[j][:3] == (kind, l, c0), (wlist[j], kind, l, c0)
        while wstate["next_load"] < min(j - hold + NSLOT, len(wlist)):
            w_emit_load(wstate["next_load"])
            wstate["next_load"] += 1
        wstate["next_use"] += 1
        slot = j % NSLOT
        return w_view(slot, kind, wlist[j][3]), bws[slot]

    P.op("pool", lambda e: e.memset(ident[:], 0.0), [], [bident])
    P.op("pool", lambda e: e.affine_select(out=ident[:], in_=ident[:], pattern=[[-1, 128]],
                                           compare_op=ALU.not_equal, fill=1.0, base=0,
                                           channel_multiplier=1), [bident], [bident])
    dve_memset(inv1024[:], 1.0 / 1024.0, [binv])
    dve_memset(inv512[:], 1.0 / 512.0, [binv])
    dve_memset(fence_cell[:], 0.0, [bfence])
    for g, wd in enumerate(WINDOWS):
        dve_memset(invc[:, g, :], 1.0 / wd, [binvc])
        for tcol in range(wd - 1):
            dve_memset(invc[:, g, tcol:tcol + 1], 1.0 / (tcol + 1), [binvc])
    dve_memset(halo[:], 0.0, [bhalo[0], bhalo[1]])

    s0, bs0 = STG.next()
    vrows = stg[0:64, s0, 0:128]
    dma("sp", stg[0:16, s0, 0:128], g_mix.rearrange("l (c p) -> (l c) p", p=128), [], [bs0], ("c", 0))
    dma("sp", stg[16:32, s0, 0:128], g_ffn.rearrange("l (c p) -> (l c) p", p=128), [], [bs0], ("c", 1))
    dma("sp", stg[32:48, s0, 0:128], g_branch.rearrange("l (c p) -> (l c) p", p=128), [], [bs0], ("c", 2))
    dma("sp", stg[48:56, s0, 0:128], g_final.rearrange("l (c p) -> (l c) p", p=128), [], [bs0], ("c", 3))
    dma("sp", stg[56:64, s0, 0:128], pool_scale.rearrange("l (c p) -> (l c) p", p=128), [], [bs0], ("c", 0))
    tr, btr = TR.next()
    tr_op(tr[:, 0:64], vrows, ident[0:64, 0:64], [bs0], [btr])
    dve_copy(vecT[:, :], tr[:, 0:64], [btr], [bvec])
    for l in range(2):
        dma("sp", gvb[:, l, :], g_v[l:l + 1, :].partition_broadcast(128), [], [bgvb], ("c", 1))
        for c in range(4):
            for half in range(2):
                hh = 2 * c + half
                dma("sp", biasb[half * 64:(half + 1) * 64, l, c, :],
                    b_sp[l, hh:hh + 1, :].partition_broadcast(64), [], [bbias], ("c", 2 + half))
    for l in range(2):
        s1, bs1 = STG.next()
        wsn = stg[:, s1, :].rearrange("p (h j) -> p h j", h=8)
        dma("sp", wsn, w_sp[l].rearrange("h i j -> i h j"), [], [bs1], ("c", 0))
        dve_memset(wsn[0:64, :, 64:128], 0.0, [bs1])
        for half in range(2):
            tr, btr = TR.next()
            trv = tr[:, :].rearrange("p (j f) -> p j f", j=4)
            for j in range(4):
                tr_op(trv[:, j, :], wsn[:, half * 4 + j, :], ident[:, :], [bs1], [btr])
            dve_copy(WsT[:, l, half * 4:half * 4 + 4, :], trv, [btr], [bwst])
        dma("pool", wpl[:, l, :, :], w_pool[l].rearrange("g c d -> c g d"), [], [bwpl], ("c", 4))
    if SAMPLE:
        for l in range(2):
            s1, bs1 = STG.next()
            dma("sp", stg[0:HIST, s1, 0:512], sst[l], [], [bs1], ("c", 1))
            tr, btr = TR.next()
            trv = tr[:, :].rearrange("p (j f) -> p j f", j=4)
            for c in range(4):
                tr_op(trv[:, c, 0:HIST], stg[0:HIST, s1, c * 128:(c + 1) * 128], ident[0:HIST, 0:HIST],
                      [bs1], [btr])
            dve_copy(halos[:, l, :, 0:HIST], trv[:, :, 0:HIST], [btr], [bhalos[l]])
    for b in (bident, binv, bvec, bgvb, bbias, bwst, bwpl, binvc):
        b.const = True

    pend = []

    def flush():
        for f in pend:
            f()
        del pend[:]

    def cols(g):
        return slice(g.c0, g.c0 + g.w)

    def stats_sq(g, src_ap, src_buf, st, bst, first, last, invt, defer=True):
        w = g.w
        s, bs = SQ.next()
        act_op(sqr[:, s, 0:w], src_ap, AF.Square, [src_buf], [bs])

        def f():
            mm_op(st[:, 0:w], invt[:, :], sqr[:, s, 0:w], first, last, [bs, binv], [bst])
        if defer:
            pend.append(f)
        else:
            f()

    def rstd_finish(g, st, bst):
        w = g.w
        r, br = RS.next()
        act_op(rsd[:, r, 0:w], st[:, 0:w], AF.Ln, [bst], [br], bias=eps_ap[:, 0:1])
        act_op(rsd[:, r, 0:w], rsd[:, r, 0:w], AF.Exp, [br], [br], scale=-0.5)
        AUX.release(bst)
        return r, br

    def norm_apply(g, src_ap, src_buf, n, r, br, gcol, dst_ap, dst_buf):
        w = g.w
        for c in range(n):
            dve_stt(dst_ap(c), src_ap(c), gcol(c), rsd[:, r, 0:w], ALU.mult, ALU.mult,
                    [src_buf(c), br, bvec], [dst_buf(c)])

    def prep_norm(g, st, gcol, dst_ap, dst_buf):
        flush()
        r, br = rstd_finish(g, *st)
        norm_apply(g, lambda c: xT[:, c, cols(g)], lambda c: bx[c][g.gi], KC, r, br, gcol, dst_ap, dst_buf)

    def prep_norm1(l):
        return lambda g, st: prep_norm(g, st, lambda c: gmix(l, c), lambda c: hT[:, c, cols(g)],
                                       lambda c: bh[c][g.gi])

    def prep_ffn(l):
        return lambda g, st: prep_norm(g, st, lambda c: gffn(l, c), lambda c: hT[:, c, cols(g)],
                                       lambda c: bh[c][g.gi])

    def prep_final():
        return lambda g, st: prep_norm(g, st, lambda c: gfin(c), lambda c: xT[:, c, cols(g)],
                                       lambda c: bx[c][g.gi])

    def full_stats(g):
        st, bst = AUX.next(hold=True)
        for c in range(KC):
            stats_sq(g, xT[:, c, cols(g)], bx[c][g.gi], st, bst, c == 0, c == KC - 1, inv1024, defer=False)
        return st, bst

    prefetched = {}

    def load_dma(g, b, hold=False):
        src = xp if g.kind == "p" else xs
        bw = g.bw
        r0 = g.row0 + b * 128
        s, bs = STG.next(hold=hold)
        dma("pool", stg[0:bw, s, :], src[r0:r0 + bw, :], [], [bs], ("ld", s))
        return s, bs

    def prefetch_loads(nxt, nblocks=4):
        if nxt is None:
            return
        n = 0
        for g in nxt.groups:
            for b in range(g.nblk):
                if n < nblocks:
                    prefetched[(id(nxt), g.gi, b)] = load_dma(g, b, hold=True)
                    n += 1

    def load_block(t, g, b):
        bw = g.bw
        key = (id(t), g.gi, b)
        if key in prefetched:
            s, bs = prefetched.pop(key)
            STG.release(bs)
        else:
            s, bs = load_dma(g, b)
        col = g.c0 + b * 128
        for half in range(2):
            tr, btr = AUX.next()
            trv = tr[:, :].rearrange("p (j f) -> p j f", j=4)
            for j in range(4):
                c = half * 4 + j
                tr_op(trv[:, j, 0:bw], stg[0:bw, s, c * 128:(c + 1) * 128], ident[0:bw, 0:bw],
                      [bs], [btr])
            dve_copy(xT[:, half * 4:half * 4 + 4, col:col + bw], trv[:, :, 0:bw],
                     [btr], [bx[half * 4 + j][g.gi] for j in range(4)])

    def load_tile(t):
        sts = {}
        for g in t.groups:
            for b in range(g.nblk):
                load_block(t, g, b)
        for g in t.groups:
            sts[g.gi] = full_stats(g)
        return sts

    def mixer(t, l, st_in, done):
        G = t.groups
        flush()
        for g in G:
            if g.gi not in done:
                prep_norm1(l)(g, st_in[g.gi])
        for g in G:
            hcols = slice(g.xb0 - HIST, g.xb0)
            if g.halo_src == "halos":
                dve_copy(xbT[:, :, hcols], halos[:, l, :, 0:HIST], [bhalos[l]], [bxbh2])
            elif g.halo_src == "zero":
                dve_memset(xbT[:, :, hcols], 0.0, [bxbh])
            elif g.halo_src == "halo":
                dve_copy(xbT[:, :, hcols], halo[:, l, :, 0:HIST], [bhalo[l]], [bxbh])
        wv, bwv = w_acquire("win", l, 512)
        wx, bwx = w_acquire("win", l, 1024, hold=1)
        for g in G:
            sv, bsv = SSV.next()
            bw = g.bw
            nb_ = g.nblk
            vts = []
            for b in range(nb_):
                col = g.c0 + b * 128
                mm, bmm = MM.next()
                for k in range(KC):
                    mm_op(mm[0:bw, :], hT[:, k, col:col + bw], wv[:, k, :], k == 0, k == KC - 1,
                          [bh[k][g.gi], bwv], [bmm])
                vt, bvt = VT.next()
                vts.append((vt, bvt))
                act_op(vtmp[0:bw, vt, :], mm[0:bw, :], AF.Gelu, [bmm], [bvt])
                s, bs = SQ.next()
                act_op(sqr[0:bw, s, :], vtmp[0:bw, vt, :], AF.Square, [bvt], [bs, bsv],
                       accum_out=ssv[0:bw, sv * 4 + b:sv * 4 + b + 1])
            P.op("dve", lambda e, sv=sv, bw=bw, nb_=nb_: e.tensor_scalar(
                out=rv[0:bw, sv * 4:sv * 4 + nb_], in0=ssv[0:bw, sv * 4:sv * 4 + nb_], scalar1=1.0 / DG,
                scalar2=EPS, op0=ALU.mult, op1=ALU.add), [bsv], [bsv])
            act_op(rv[0:bw, sv * 4:sv * 4 + nb_], rv[0:bw, sv * 4:sv * 4 + nb_], AF.Ln, [bsv], [bsv])
            act_op(rv2[0:bw, sv * 4:sv * 4 + nb_], rv[0:bw, sv * 4:sv * 4 + nb_], AF.Exp, [bsv], [bsv],
                   scale=-0.5)
            for b in range(nb_):
                vt, bvt = vts[b]
                bi = g.gi * 4 + b
                rcol = rv2[0:bw, sv * 4 + b:sv * 4 + b + 1]
                if g.kind == "s":
                    dve_stt(vout[0:bw, :], vtmp[0:bw, vt, :], rcol, gvb[0:bw, l, :],
                            ALU.mult, ALU.mult, [bvt, bsv, bgvb], [bvout])
                    dve_copy(vn[0:bw, bi, :], vout[0:bw, :], [bvout], [bvn[bi]])
                    dma("sp", vs[l, 0:bw, :], vout[0:bw, :], [bvout], [], ("vout", 0), is_out=True)
                else:
                    dve_stt(vn[0:bw, bi, :], vtmp[0:bw, vt, :], rcol, gvb[0:bw, l, :],
                            ALU.mult, ALU.mult, [bvt, bsv, bgvb], [bvn[bi]])
            for c in range(4):
                mm, bmm = MM.next()
                for k in range(KC):
                    mm_op(mm[:, 0:g.w], wx[:, k, c * 128:(c + 1) * 128], hT[:, k, cols(g)],
                          k == 0, k == KC - 1, [bh[k][g.gi], bwx], [bmm])
                act_op(xbT[:, c, g.xb0:g.xb0 + g.w], mm[:, 0:g.w], AF.Copy,
                       [bmm], [bxb[c][g.gi]])
        for gl in G:
            if gl.halo_save:
                dve_copy(halo[:, l, :, 0:HIST], xbT[:, :, gl.xb0 + gl.w - HIST:gl.xb0 + gl.w],
                         [bxb[c][gl.gi] for c in range(4)], [bhalo[l]])
        for gl in G:
            if gl.hist_out is None:
                continue
            tr, btr = AUX.next()
            trv = tr[:, :].rearrange("p (j f) -> p j f", j=4)
            for c in range(4):
                tr_op(trv[0:HIST, c, :], xbT[:, c, gl.xb0 + gl.w - HIST:gl.xb0 + gl.w], ident[:, :],
                      [bxb[c][gl.gi]], [btr])
            dve_copy(hst[0:HIST, :].rearrange("p (j f) -> p j f", j=4), trv[0:HIST, :, :], [btr], [bhst])
            dst = hp[l, gl.hist_out[1]] if gl.hist_out[0] == "p" else hs[l]
            dma("sp", dst, hst[0:HIST, :], [bhst], [], ("hst", 0), is_out=True)
        def pool_dve(g, gi):
            w = g.w
            wd = WINDOWS[gi]
            if True:
                base = g.xb0

                def X(a, b):
                    return xbT[:, gi, base + a:base + b]

                prevb = {"prev": bxb[gi][g.gi - 1], "halos": bxbh2}.get(g.halo_src, bxbh)
                rd = [bxb[gi][g.gi], prevb]
                ext = wd - 2
                pool_tt(ptmp[:, 0, 0:w + ext], X(-ext, w), X(-ext - 1, w - 1), ALU.add, rd, [bpt[0]])
                cur = 0
                span = 2
                while span < wd:
                    ext2 = wd - 2 * span
                    nxt = (cur + 1) % 3
                    pool_tt(ptmp[:, nxt, 0:w + ext2], ptmp[:, cur, span:span + w + ext2],
                           ptmp[:, cur, 0:w + ext2], ALU.add, [bpt[cur]], [bpt[nxt]])
                    cur = nxt
                    span *= 2
                nx2 = (cur + 1) % 3
                pool_tt(ptmp[:, nx2, 0:w], ptmp[:, cur, 0:w], invc[:, gi, 15:16].to_broadcast([128, w]),
                        ALU.mult, [bpt[cur], binvc], [bpt[nx2]])
                pool_tt(dT[:, gi, cols(g)], ptmp[:, nx2, 0:w], X(0, w), ALU.subtract,
                        [bpt[nx2], bxb[gi][g.gi]], [bd[gi][g.gi]])
                if g.first:
                    pool_tt(tiny[:, 0:16], ptmp[:, cur, 0:16], invc[:, gi, :], ALU.mult,
                           [bpt[cur], binvc], [btiny])
                    pool_tt(dT[:, gi, g.c0:g.c0 + 16], tiny[:, 0:16], X(0, 16), ALU.subtract,
                           [btiny, bxb[gi][g.gi]], [bd[gi][g.gi]])
        for g in G:
            for gi in range(4):
                pool_dve(g, gi)
        wu, bwu = w_acquire("win", l, 0)
        sts_a = {}
        sts_b = {}
        pool_q = []

        def pool_mm(g, gi, stb, bstb):
            w = g.w
            mm, bmm = MM.next()
            mm_op(mm[:, 0:w], wpl[:, l, gi, :], dT[:, gi, cols(g)], True, True,
                  [bd[gi][g.gi], bwpl], [bmm])
            flush()
            act_op(ybT[:, gi, cols(g)], mm[:, 0:w], AF.Copy, [bmm, bvec], [byb[gi][g.gi]],
                   scale=psc(l, gi))
            if not g.small:
                stats_sq(g, ybT[:, gi, cols(g)], byb[gi][g.gi], stb, bstb, gi == 0, gi == 3, inv512)

        def pool_drain(g=None, n=None):
            k = 0
            while pool_q and (g is None or pool_q[0][0] is g) and (n is None or k < n):
                pool_q.pop(0)[1]()
                k += 1

        for g in G:
            w = g.w
            bw = g.bw
            if g.small:
                sta = bsta = stb = bstb = None
            else:
                sta, bsta = AUX.next(hold=True)
                stb, bstb = AUX.next(hold=True)
                sts_a[g.gi] = (sta, bsta)
                sts_b[g.gi] = (stb, bstb)
            for c in range(4):
                mmx, bmx = MM.next()
                for half in range(2):
                    hh = 2 * c + half
                    for b in range(g.nblk):
                        bi = g.gi * 4 + b
                        mm_op(mmx[half * 64:(half + 1) * 64, b * 128:b * 128 + bw],
                              vn[0:bw, bi, hh * 64:(hh + 1) * 64], WsT[0:bw, l, hh, 0:bw], True, True,
                              [bvn[bi], bwst], [bmx])
                mmy, bmy = MM.next()
                for k in range(KC):
                    mm_op(mmy[:, 0:w], wu[:, k, c * 128:(c + 1) * 128], hT[:, k, cols(g)],
                          k == 0, k == KC - 1, [bh[k][g.gi], bwu], [bmy])
                flush()
                pool_drain(n=1)
                ut, but = UT.next()
                act_op(utmp[:, ut, 0:w], mmy[:, 0:w], AF.Gelu, [bmy], [but])
                tt, btt = TTR.next()
                if g.nblk > 1:
                    dve_tt(ttmp[:, tt, 0:w].rearrange("p (b i) -> p b i", b=g.nblk),
                           mmx[:, 0:w].rearrange("p (b i) -> p b i", b=g.nblk),
                           biasb[:, l, c, :].unsqueeze(1).to_broadcast([128, g.nblk, 128]),
                           ALU.add, [bmx, bbias], [btt])
                else:
                    dve_tt(ttmp[:, tt, 0:w], mmx[:, 0:w], biasb[:, l, c, 0:w], ALU.add, [bmx, bbias], [btt])
                dve_tt(yaT[:, c, cols(g)], ttmp[:, tt, 0:w], utmp[:, ut, 0:w], ALU.mult,
                       [btt, but], [bya[c][g.gi]])
                if not g.small:
                    stats_sq(g, yaT[:, c, cols(g)], bya[c][g.gi], sta, bsta, c == 0, c == 3, inv512)
            for gi in range(4):
                pool_q.append((g, lambda g=g, gi=gi, stb=stb, bstb=bstb: pool_mm(g, gi, stb, bstb)))
        wo = [None, None]
        bwo = [None, None]
        wo[0], bwo[0] = w_acquire("wout", l, 0)
        wo[1], bwo[1] = w_acquire("wout", l, 512, hold=1)
        st_out = {}
        done_ffn = set()

        def prep_wout(g):
            pool_drain(g=g)
            flush()
            if g.small:
                sta, bsta = AUX.next(hold=True)
                for c in range(4):
                    stats_sq(g, yaT[:, c, cols(g)], bya[c][g.gi], sta, bsta, c == 0, c == 3, inv512, defer=False)
                ra, bra = rstd_finish(g, sta, bsta)
                stb, bstb = AUX.next(hold=True)
                for c in range(4):
                    stats_sq(g, ybT[:, c, cols(g)], byb[c][g.gi], stb, bstb, c == 0, c == 3, inv512, defer=False)
                rb, brb = rstd_finish(g, stb, bstb)
            else:
                ra, bra = rstd_finish(g, *sts_a[g.gi])
                rb, brb = rstd_finish(g, *sts_b[g.gi])
            norm_apply(g, lambda c: yaT[:, c, cols(g)], lambda c: bya[c][g.gi], 4, ra, bra,
                       lambda c: gbr(l, c), lambda c: hT[:, c, cols(g)], lambda c: bh[c][g.gi])
            norm_apply(g, lambda c: ybT[:, c, cols(g)], lambda c: byb[c][g.gi], 4, rb, brb,
                       lambda c: gbr(l, 4 + c), lambda c: hT[:, 4 + c, cols(g)], lambda c: bh[4 + c][g.gi])

        prep_wout(G[0])
        for k, g in enumerate(G):
            st, bst = AUX.next(hold=True)
            st_out[g.gi] = (st, bst)
            u = 0
            for ch in range(2):
                for oc in range(4):
                    o = ch * 4 + oc
                    mm, bmm = MM.next()
                    for kk in range(KC):
                        mm_op(mm[:, 0:g.w], wo[ch][:, kk, oc * 128:(oc + 1) * 128], hT[:, kk, cols(g)],
                              kk == 0, kk == KC - 1, [bh[kk][g.gi], bwo[ch]], [bmm])
                    flush()
                    pool_drain(n=1)
                    dve_tt(xT[:, o, cols(g)], xT[:, o, cols(g)], mm[:, 0:g.w], ALU.add,
                           [bx[o][g.gi], bmm], [bx[o][g.gi]])
                    stats_sq(g, xT[:, o, cols(g)], bx[o][g.gi], st, bst, o == 0, o == 7, inv1024)
                    u += 1
                    if u == 2 and k >= 1:
                        pg = G[k - 1]
                        prep_ffn(l)(pg, st_out[pg.gi])
                        done_ffn.add(pg.gi)
                    if u == 5 and k + 1 < len(G):
                        prep_wout(G[k + 1])
        return st_out, done_ffn

    def ffn(t, l, st_in, done, next_prep, nxt_tile=None):
        G = t.groups
        for g in G:
            if g.gi not in done:
                prep_ffn(l)(g, st_in[g.gi])
        for j in range(6):
            nco = 512 if j < 5 else 256
            wg, bwg = w_acquire("gate", l, j * 512)
            wu, bwu = w_acquire("up", l, j * 512, hold=1)
            for g in G:
                w = g.w
                for fc in range(nco // 128):
                    f = j * 4 + fc
                    ma, bma = MM.next()
                    for k in range(KC):
                        mm_op(ma[:, 0:w], wg[:, k, fc * 128:(fc + 1) * 128], hT[:, k, cols(g)],
                              k == 0, k == KC - 1, [bh[k][g.gi], bwg], [bma])
                    mb, bmb = MM.next()
                    for k in range(KC):
                        mm_op(mb[:, 0:w], wu[:, k, fc * 128:(fc + 1) * 128], hT[:, k, cols(g)],
                              k == 0, k == KC - 1, [bh[k][g.gi], bwu], [bmb])
                    sm, bsm = STM.next()
                    act_op(stmp[:, sm, 0:w], ma[:, 0:w], AF.Silu, [bma], [bsm])
                    dve_tt(actT[:, f, cols(g)], stmp[:, sm, 0:w], mb[:, 0:w], ALU.mult,
                           [bsm, bmb], [bact[f][g.gi]])
        st_out = {}
        done_next = set()
        prefetch_loads(nxt_tile)
        for g in G:
            st_out[g.gi] = AUX.next(hold=True)
        for op_ in range(4):
            wd_ = [None, None]
            bwd = [None, None]
            wd_[0], bwd[0] = w_acquire("down", l, (2 * op_) * 128)
            wd_[1], bwd[1] = w_acquire("down", l, (2 * op_ + 1) * 128, hold=1)
            for kg, g in enumerate(G):
                st, bst = st_out[g.gi]
                for i2 in range(2):
                    o = 2 * op_ + i2
                    mm, bmm = MM.next()
                    for k in range(FC):
                        mm_op(mm[:, 0:g.w], wd_[i2][:, k, :], actT[:, k, cols(g)], k == 0, k == FC - 1,
                              [bact[k][g.gi], bwd[i2]], [bmm])
                    flush()
                    dve_tt(xT[:, o, cols(g)], xT[:, o, cols(g)], mm[:, 0:g.w], ALU.add,
                           [bx[o][g.gi], bmm], [bx[o][g.gi]])
                    stats_sq(g, xT[:, o, cols(g)], bx[o][g.gi], st, bst, o == 0, o == 7, inv1024)
                    if op_ == 3 and i2 == 1 and kg >= 1:
                        pg = G[kg - 1]
                        next_prep(pg, st_out[pg.gi])
                        done_next.add(pg.gi)
        return st_out, done_next

    def final_block(t, g, b):
        dst = yp if g.kind == "p" else ys
        bw = g.bw
        col = g.c0 + b * 128
        r0 = g.row0 + b * 128
        s, bs = STG.next()
        for half in range(2):
            tr, btr = AUX.next()
            trv = tr[:, :].rearrange("p (j f) -> p j f", j=4)
            for j in range(4):
                c = half * 4 + j
                tr_op(trv[0:bw, j, :], xT[:, c, col:col + bw], ident[:, :], [bx[c][g.gi]], [btr])
            act_op(stg[0:bw, s, half * 512:(half + 1) * 512].rearrange("p (j f) -> p j f", j=4),
                   trv[0:bw, :, :], AF.Copy, [btr], [bs])
        dma("sp", dst[r0:r0 + bw, :], stg[0:bw, s, :], [bs], [], ("stg", s), is_out=True)

    def final_and_load(t, st_in, nxt, done):
        for g in t.groups:
            if g.gi not in done:
                prep_final()(g, st_in[g.gi])
        ngroups = {g.gi: g for g in (nxt.groups if nxt is not None else [])}
        pendL = []
        for g in t.groups:
            for b in range(g.nblk):
                final_block(t, g, b)
                if pendL:
                    ng, nb = pendL.pop(0)
                    load_block(nxt, ng, nb)
            if g.gi in ngroups:
                ng = ngroups.pop(g.gi)
                pendL += [(ng, nb) for nb in range(ng.nblk)]
        for ng, nb in pendL:
            load_block(nxt, ng, nb)
        for ng in ngroups.values():
            for nb in range(ng.nblk):
                load_block(nxt, ng, nb)
        sts = {}
        if nxt is not None:
            for g in nxt.groups:
                sts[g.gi] = full_stats(g)
        return sts

    eps_ap = sb("eps_ap", [128, 1], F32)
    beps = Buf("eps")
    dve_memset(eps_ap[:], EPS, [beps])
    beps.const = True
    P.op("act", lambda e: e.activation(out=fence_cell[:, 1:2], in_=eps_ap[:, 0:1], func=AF.Copy),
         [beps], [bfence])

    st = load_tile(tiles[0])
    for ti, t in enumerate(tiles):
        done = set()
        for l in range(NLAYER):
            fence()
            st, done = mixer(t, l, st, done)
            if DEBUG and t is tiles[0] and l == 0:
                dma("sp", dbg_x1, xT[:, :, :], [b2 for row in bx for b2 in row], [], ("dbg", 4), is_out=True)
            fence()
            nprep = prep_norm1(l + 1) if l + 1 < NLAYER else prep_final()
            nxt_t = tiles[ti + 1] if (l + 1 == NLAYER and ti + 1 < len(tiles)) else None
            st, done = ffn(t, l, st, done, nprep, nxt_t)
        fence()
        st = final_and_load(t, st, tiles[ti + 1] if ti + 1 < len(tiles) else None, done)
    flush()

    assert wstate["next_use"] == len(wlist)
    P.finalize(nc)
    return nc, P


_CACHE = {}


def kernel(x_prompt, x_sample, state_pool, g_mix, w_in, g_v, w_spatial, b_spatial, w_pool, pool_scale,
           g_branch, w_out, g_ffn, w_gate, w_up, w_down, g_final):
    f = lambda a: np.ascontiguousarray(np.asarray(a, dtype=np.float32))
    x_prompt = f(x_prompt)
    x_sample = f(x_sample)
    state_pool = f(state_pool)
    B, S, _ = x_prompt.shape
    nseq = B // N_CORES
    if "nc" not in _CACHE:
        _CACHE["nc"] = build_program(NSEQ=nseq, SEQ=S, SAMPLE=True)[0]
    nc = _CACHE["nc"]
    shared = {
        "g_mix": f(g_mix), "w_in": f(w_in), "g_v": f(g_v), "w_sp": f(w_spatial), "b_sp": f(b_spatial),
        "w_pool": f(w_pool), "pool_scale": f(pool_scale), "g_branch": f(g_branch), "w_out": f(w_out),
        "g_ffn": f(g_ffn), "w_gate": f(w_gate), "w_up": f(w_up), "w_down": f(w_down),
        "g_final": f(g_final).reshape(1, D),
    }
    in_maps = []
    for i in range(N_CORES):
        m = dict(shared)
        m["xp"] = x_prompt[i * nseq:(i + 1) * nseq].reshape(nseq * S, D)
        m["xs"] = x_sample[i]
        m["sst"] = np.ascontiguousarray(state_pool[:, i])
        in_maps.append(m)
    res = run_bass_kernel_spmd(nc, in_maps, core_ids=list(range(N_CORES)))
    rs = res.results
    y_prompt = np.concatenate([r["yp"].reshape(nseq, S, D) for r in rs], axis=0)
    y_sample = np.stack([r["ys"] for r in rs], axis=0)
    hp = np.concatenate([r["hp"] for r in rs], axis=1)
    hs = np.stack([r["hs"] for r in rs], axis=1)
    vs = np.stack([r["vs"] for r in rs], axis=1)
    return (y_prompt.astype(np.float32), y_sample.astype(np.float32), hp.astype(np.float32),
            hs.astype(np.float32), vs.astype(np.float32))
```

```python
import numpy as np
import concourse.bass as bass
import concourse.mybir as mybir
from concourse.bass_utils import run_bass_kernel_spmd

F32 = mybir.dt.float32
BF16 = mybir.dt.bfloat16
AF = mybir.ActivationFunctionType
ALU = mybir.AluOpType

EPS = 1e-6
D = 1024
DG = 512
DFF = 2816
KC = 8
FC = 22
HIST = 15
WINDOWS = (2, 4, 8, 16)
N_CORES = 8


class Buf:
    __slots__ = ("name", "w", "rs", "const", "tok")

    def __init__(self, name, tok=None):
        self.name = name
        self.w = None
        self.rs = {}
        self.const = False
        self.tok = tok


class Op:
    __slots__ = ("eng", "fn", "deps", "lane", "signal", "sig", "clock", "waits", "idx")


class Prog:
    ENGS = ("pe", "act", "dve", "pool", "sp")

    def __init__(self):
        self.ops = []
        self.lane_last = {}
        self.out_ops = []

    def op(self, eng, fn, reads=(), writes=(), lane=None, is_out=False):
        o = Op()
        o.eng = eng
        o.fn = fn
        o.lane = lane
        o.signal = lane is not None
        o.idx = len(self.ops)
        o.sig = None
        deps = {}
        reads = list(reads)
        for b in list(reads) + list(writes):
            if b.tok is not None and b.tok not in reads:
                reads.append(b.tok)

        def add(d):
            if d is None:
                return
            if d.lane is None and d.eng == "pe" and eng == "pe" and lane is None:
                return
            key = d.eng if d.lane is None else ("lane", d.lane)
            cur = deps.get(key)
            if cur is None or cur.idx < d.idx:
                deps[key] = d

        for b in reads:
            add(b.w)
        for b in writes:
            add(b.w)
            for r in b.rs.values():
                add(r)
        if lane is not None:
            add(self.lane_last.get(lane))
            self.lane_last[lane] = o
        mykey = eng if lane is None else ("lane", lane)
        for b in reads:
            if not b.const:
                b.rs[mykey] = o
        for b in writes:
            b.w = o
            b.rs = {}
        o.deps = sorted(deps.values(), key=lambda d: d.idx)
        for d in o.deps:
            d.signal = True
        self.ops.append(o)
        if is_out:
            self.out_ops.append(o)
        return o

    def finalize(self, nc):
        fin = Op()
        fin.eng = "sp"
        fin.fn = None
        fin.lane = None
        fin.signal = False
        fin.idx = len(self.ops)
        fin.sig = None
        dd = {}
        for d in self.out_ops:
            key = ("lane", d.lane)
            if key not in dd or dd[key].idx < d.idx:
                dd[key] = d
        fin.deps = sorted(dd.values(), key=lambda d: d.idx)
        self.ops.append(fin)

        eng_count = {e: 0 for e in self.ENGS}
        lane_count = {}
        known = {e: {} for e in self.ENGS}
        nwaits = 0
        for o in self.ops:
            kn = known[o.eng]
            waits = {}
            for d in o.deps:
                key, val = d.sig
                if kn.get(key, 0) >= val:
                    continue
                if waits.get(key, 0) < val:
                    waits[key] = val
                for k2, v2 in d.clock.items():
                    if kn.get(k2, 0) < v2:
                        kn[k2] = v2
            o.waits = list(waits.items())
            nwaits += len(o.waits)
            if o.lane is not None:
                lane_count[o.lane] = lane_count.get(o.lane, 0) + 16
                o.sig = (("lane", o.lane), lane_count[o.lane])
            elif o.signal:
                eng_count[o.eng] += 1
                o.sig = (o.eng, eng_count[o.eng])
            if o.sig is not None:
                o.clock = dict(kn)
                o.clock[o.sig[0]] = o.sig[1]
            else:
                o.clock = None
        self.stats = dict(n_ops=len(self.ops), n_waits=nwaits, eng_count=eng_count,
                          n_lanes=len(lane_count))

        sems = {}
        for e in self.ENGS:
            sems[e] = nc.alloc_semaphore("s_" + e)
        for i, ln in enumerate(lane_count):
            sems[("lane", ln)] = nc.alloc_semaphore("l_%d" % i)

        per_eng = {e: [] for e in self.ENGS}
        for o in self.ops:
            per_eng[o.eng].append(o)

        def emit(ename, e):
            for o in per_eng[ename]:
                for key, val in o.waits:
                    e.wait_ge(sems[key], val)
                if o.fn is None:
                    continue
                ins = o.fn(e)
                if o.sig is not None:
                    ins.then_inc(sems[o.sig[0]], 16 if o.lane is not None else 1)

        with nc.Block() as block:
            @block.tensor
            def _(e):
                emit("pe", e)

            @block.scalar
            def _(e):
                emit("act", e)

            @block.vector
            def _(e):
                emit("dve", e)

            @block.gpsimd
            def _(e):
                emit("pool", e)

            @block.sync
            def _(e):
                emit("sp", e)


class Ring:
    def __init__(self, items):
        self.items = items
        self.i = 0

    def next(self):
        it = self.items[self.i % len(self.items)]
        self.i += 1
        return it


class LiveRing:
    def __init__(self, items):
        self.items = items
        self.live = [False] * len(items)
        self.i = 0

    def next(self, hold=False):
        for _ in range(len(self.items)):
            k = self.i % len(self.items)
            self.i += 1
            if not self.live[k]:
                self.live[k] = hold
                return self.items[k]
        raise RuntimeError("LiveRing exhausted")

    def release(self, buf):
        for k, it in enumerate(self.items):
            if it[1] is buf:
                self.live[k] = False
                return
        raise KeyError(buf.name)


class Group:
    def __init__(self, gi, c0, w, kind, row0, xb0, halo_src, first=False, halo_save=False, hist_out=None):
        self.gi = gi
        self.c0 = c0
        self.w = w
        self.nblk = (w + 127) // 128
        self.bw = min(128, w)
        self.kind = kind
        self.row0 = row0
        self.xb0 = xb0
        self.halo_src = halo_src
        self.first = first
        self.halo_save = halo_save
        self.hist_out = hist_out
        self.small = w < 512


class Tile:
    pass


def build_program(NSEQ=2, SEQ=4096, SAMPLE=True, NLAYER=2, DEBUG=False):
    nc = bass.Bass("TRN2", target_bir_lowering=False)
    P = Prog()
    TT = 1024
    NSW = 64
    TW = TT + NSW

    def din(name, shape):
        return nc.dram_tensor(name, list(shape), F32, kind="ExternalInput").ap()

    def dout(name, shape):
        return nc.dram_tensor(name, list(shape), F32, kind="ExternalOutput").ap()

    xp = din("xp", [NSEQ * SEQ, D])
    xs = din("xs", [NSW, D])
    sst = din("sst", [2, HIST, DG])
    g_mix = din("g_mix", [2, D])
    w_in = din("w_in", [2, D, 1536])
    g_v = din("g_v", [2, DG])
    w_sp = din("w_sp", [2, 8, 128, 128])
    b_sp = din("b_sp", [2, 8, 128])
    w_pool = din("w_pool", [2, 4, 128, 128])
    pool_scale = din("pool_scale", [2, DG])
    g_branch = din("g_branch", [2, D])
    w_out = din("w_out", [2, D, D])
    g_ffn = din("g_ffn", [2, D])
    w_gate = din("w_gate", [2, D, DFF])
    w_up = din("w_up", [2, D, DFF])
    w_down = din("w_down", [2, DFF, D])
    g_final = din("g_final", [1, D])

    yp = dout("yp", [NSEQ * SEQ, D])
    ys = dout("ys", [NSW, D])
    hp = dout("hp", [2, NSEQ, HIST, DG])
    hs = dout("hs", [2, HIST, DG])
    vs = dout("vs", [2, NSW, DG])
    if DEBUG:
        dbg_ya = dout("dbg_ya", [128, 4, 1024])
        dbg_yb = dout("dbg_yb", [128, 4, 1024])
        dbg_ym = nc.dram_tensor("dbg_ym", [128, 8, 1024], BF16, kind="ExternalOutput").ap()
        dbg_x1 = dout("dbg_x1", [128, 8, 1024])
        dbg_x2 = dout("dbg_x2", [128, 8, 1024])
        dbg_d = nc.dram_tensor("dbg_d", [128, 4, 1024], BF16, kind="ExternalOutput").ap()
        dbg_act = nc.dram_tensor("dbg_act", [128, 22, 1024], BF16, kind="ExternalOutput").ap()
        dbg_h2 = nc.dram_tensor("dbg_h2", [128, 8, 1024], BF16, kind="ExternalOutput").ap()

    def sb(name, shape, dt):
        return nc.alloc_sbuf_tensor(name, list(shape), dt)

    def ps(name):
        return nc.alloc_psum_tensor(name, [128, 512], F32)

    xT = sb("xT", [128, KC, TW], F32)
    hT = sb("hT", [128, KC, TW], BF16)
    sqr = sb("sqr", [128, 4, 512], BF16)
    rsd = sb("rsd", [128, 3, 512], F32)
    NSLOT = 4
    wsl = sb("wsl", [128, NSLOT, 4096], BF16)
    ident = sb("ident", [128, 128], F32)
    inv1024 = sb("inv1024", [128, 128], BF16)
    inv512 = sb("inv512", [128, 128], BF16)
    vecT = sb("vecT", [128, 64], F32)
    gvb = sb("gvb", [128, 2, 512], F32)
    biasb = sb("biasb", [128, 2, 4, 128], F32)
    WsT = sb("WsT", [128, 2, 8, 128], BF16)
    wpl = sb("wpl", [128, 2, 4, 128], BF16)
    invc = sb("invc", [128, 4, 16], F32)
    halo = sb("halo", [128, 2, 4, 16], F32)
    halos = sb("halos", [128, 2, 4, 16], F32)
    ssv = sb("ssv", [128, 8], F32)
    rv = sb("rv", [128, 8], F32)
    rv2 = sb("rv2", [128, 8], F32)
    tiny = sb("tiny", [128, 16], F32)
    hst = sb("hst", [16, 512], F32)
    vout = sb("vout", [128, 512], F32)
    fence_cell = sb("fence_cell", [128, 2], F32)

    XBW = 1120
    XB_P = HIST
    XB_S = HIST + TT + HIST
    OFF_VTMP = 0
    OFF_VN = OFF_VTMP + 2048
    OFF_XB = OFF_VN + 2304
    OFF_DT = OFF_XB + 4 * XBW
    OFF_PT = OFF_DT + 2 * TW
    OFF_UT = OFF_PT + 3 * 528
    OFF_TT = OFF_UT + 1024
    OFF_YA = OFF_TT + 1024
    OFF_YB = OFF_YA + 4 * TW
    U_END = OFF_YB + 4 * TW
    OFF_ACT = 0
    OFF_ST = 11 * TW
    assert OFF_ST + 1536 <= U_END
    U = sb("U", [128, U_END], F32)

    def uf(off, n, pat=None, **kw):
        v = U[:, off:off + n]
        if pat:
            v = v.rearrange(pat, **kw)
        return v

    def ub(off, nwords, pat=None, **kw):
        v = U[:, off:off + nwords].bitcast(BF16)
        if pat:
            v = v.rearrange(pat, **kw)
        return v

    vtmp = uf(OFF_VTMP, 2048, "p (r f) -> p r f", r=4)
    vn = ub(OFF_VN, 2304, "p (b f) -> p b f", b=9)
    xbT = uf(OFF_XB, 4 * XBW, "p (c t) -> p c t", c=4)
    dT = ub(OFF_DT, 2 * TW, "p (c t) -> p c t", c=4)
    ptmp = uf(OFF_PT, 3 * 528, "p (r f) -> p r f", r=3)
    utmp = uf(OFF_UT, 1024, "p (r f) -> p r f", r=2)
    ttmp = uf(OFF_TT, 1024, "p (r f) -> p r f", r=2)
    yaT = uf(OFF_YA, 4 * TW, "p (c t) -> p c t", c=4)
    ybT = uf(OFF_YB, 4 * TW, "p (c t) -> p c t", c=4)
    actT = ub(OFF_ACT, 11 * TW, "p (c t) -> p c t", c=FC)
    stmp = uf(OFF_ST, 1536, "p (r f) -> p r f", r=3)
    NSTG = 8
    OFF_STG = OFF_ST + 1536
    assert OFF_STG + NSTG * 1024 <= U_END
    stg = uf(OFF_STG, NSTG * 1024, "p (s f) -> p s f", s=NSTG)

    mm_banks = [ps("mm%d" % i) for i in range(4)]
    aux_banks = [ps("aux%d" % i) for i in range(4)]

    tokU = Buf("tokU")
    bx = [[Buf("x%d_%d" % (c, g)) for g in range(3)] for c in range(KC)]
    bh = [[Buf("h%d_%d" % (c, g)) for g in range(3)] for c in range(KC)]
    bxb = [[Buf("xb%d_%d" % (c, g), tokU) for g in range(3)] for c in range(4)]
    bxbh = Buf("xbh", tokU)
    bxbh2 = Buf("xbh2", tokU)
    bd = [[Buf("d%d_%d" % (c, g), tokU) for g in range(3)] for c in range(4)]
    bya = [[Buf("ya%d_%d" % (c, g), tokU) for g in range(3)] for c in range(4)]
    byb = [[Buf("yb%d_%d" % (c, g), tokU) for g in range(3)] for c in range(4)]
    bvn = [Buf("vn%d" % b, tokU) for b in range(9)]
    bact = [[Buf("a%d_%d" % (f, g), tokU) for g in range(3)] for f in range(FC)]
    MM = Ring([(mm_banks[i], Buf("mm%d" % i)) for i in range(4)])
    AUX = LiveRing([(aux_banks[i], Buf("aux%d" % i)) for i in range(4)])
    TR = AUX
    SQ = Ring([(i, Buf("sq%d" % i)) for i in range(4)])
    RS = Ring([(i, Buf("rs%d" % i)) for i in range(3)])
    STG = LiveRing([(i, Buf("stg%d" % i, tokU)) for i in range(8)])
    VT = Ring([(i, Buf("vt%d" % i, tokU)) for i in range(4)])
    UT = Ring([(i, Buf("ut%d" % i, tokU)) for i in range(2)])
    TTR = Ring([(i, Buf("tt%d" % i, tokU)) for i in range(2)])
    STM = Ring([(i, Buf("sm%d" % i, tokU)) for i in range(3)])
    SSV = Ring([(i, Buf("ssv%d" % i)) for i in range(2)])
    bpt = [Buf("pt%d" % i, tokU) for i in range(3)]
    btiny = Buf("tiny")
    bident = Buf("ident")
    binv = Buf("inv")
    bvec = Buf("vec")
    bgvb = Buf("gvb")
    bbias = Buf("bias")
    bwst = Buf("wst")
    bwpl = Buf("wpl")
    binvc = Buf("invc")
    bhalo = [Buf("halo%d" % l) for l in range(2)]
    bhalos = [Buf("halos%d" % l) for l in range(2)]
    bhst = Buf("hst")
    bvout = Buf("vout")
    bws = [Buf("ws%d" % i) for i in range(NSLOT)]
    bfence = Buf("fencecell")

    def gmix(l, c):
        return vecT[:, l * 8 + c:l * 8 + c + 1]

    def gffn(l, c):
        return vecT[:, 16 + l * 8 + c:16 + l * 8 + c + 1]

    def gbr(l, c):
        return vecT[:, 32 + l * 8 + c:32 + l * 8 + c + 1]

    def gfin(c):
        return vecT[:, 48 + c:48 + c + 1]

    def psc(l, g):
        return vecT[:, 56 + l * 4 + g:56 + l * 4 + g + 1]

    def mm_op(out, lhsT, rhs, start, stop, reads, writes):
        P.op("pe", lambda e: e.matmul(out, lhsT=lhsT, rhs=rhs, start=start, stop=stop), reads, writes)

    def tr_op(out, in_, idn, reads, writes):
        P.op("pe", lambda e: e.transpose(out, in_, idn), reads + [bident], writes)

    def act_op(out, in_, func, reads, writes, **kw):
        P.op("act", lambda e: e.activation(out=out, in_=in_, func=func, **kw), reads, writes)

    def dve_tt(out, in0, in1, op, reads, writes):
        P.op("dve", lambda e: e.tensor_tensor(out=out, in0=in0, in1=in1, op=op), reads, writes)

    def dve_stt(out, in0, scalar, in1, op0, op1, reads, writes):
        P.op("dve", lambda e: e.scalar_tensor_tensor(out=out, in0=in0, scalar=scalar, in1=in1,
                                                     op0=op0, op1=op1), reads, writes)

    def pool_tt(out, in0, in1, op, reads, writes):
        P.op("pool", lambda e: e.tensor_tensor(out=out, in0=in0, in1=in1, op=op), reads, writes)

    def pool_stt(out, in0, scalar, in1, op0, op1, reads, writes):
        P.op("pool", lambda e: e.scalar_tensor_tensor(out=out, in0=in0, scalar=scalar, in1=in1,
                                                      op0=op0, op1=op1), reads, writes)

    def dve_copy(out, in_, reads, writes):
        P.op("dve", lambda e: e.tensor_copy(out, in_), reads, writes)

    def dve_recip(out, in_, reads, writes):
        P.op("dve", lambda e: e.reciprocal(out, in_), reads, writes)

    def dve_memset(ap, val, writes):
        P.op("dve", lambda e: e.memset(ap, val), [], writes)

    def dma(eng, out, in_, reads, writes, lane, is_out=False):
        P.op(eng, lambda e: e.dma_start(out=out, in_=in_), reads, writes, lane=lane, is_out=is_out)

    def fence():
        P.op("dve", lambda e: e.memset(fence_cell[:, 0:1], 0.0), [], [tokU, bfence])

    wlist = []
    tiles = []
    for s_ in range(NSEQ):
        nt = SEQ // TT
        for ti in range(nt):
            t = Tile()
            first = ti == 0
            last = ti == nt - 1
            r0 = s_ * SEQ + ti * TT
            t.groups = [
                Group(0, 0, 512, "p", r0, XB_P, "zero" if first else "halo", first=first),
                Group(1, 512, 512, "p", r0 + 512, XB_P + 512, "prev", halo_save=not last,
                      hist_out=("p", s_) if last else None),
            ]
            tiles.append(t)
    if SAMPLE:
        tiles[-1].groups.append(Group(2, TT, NSW, "s", 0, XB_S, "halos", hist_out=("s",)))

    for t in tiles:
        for l in range(NLAYER):
            wlist.append(("win", l, 512, 512))
            wlist.append(("win", l, 1024, 512))
            wlist.append(("win", l, 0, 512))
            wlist.append(("wout", l, 0, 512))
            wlist.append(("wout", l, 512, 512))
            for j in range(6):
                nco = 512 if j < 5 else 256
                wlist.append(("gate", l, j * 512, nco))
                wlist.append(("up", l, j * 512, nco))
            for o in range(8):
                wlist.append(("down", l, o * 128, 128))
    wstate = {"next_load": 0, "next_use": 0}
    wsrc = {"win": w_in, "wout": w_out, "gate": w_gate, "up": w_up, "down": w_down}

    def w_view(slot, kind, nco):
        if kind == "down":
            return wsl[:, slot, 0:FC * 128].rearrange("p (k f) -> p k f", k=FC)
        return wsl[:, slot, 0:KC * nco].rearrange("p (k f) -> p k f", k=KC)

    def w_emit_load(i):
        kind, l, c0, nco = wlist[i]
        slot = i % NSLOT
        src = wsrc[kind][l].rearrange("(k p) f -> p k f", p=128)[:, :, c0:c0 + nco]
        dma("pool", w_view(slot, kind, nco), src, [], [bws[slot]], lane=("w", slot))

    def w_acquire(kind, l, c0, hold=0):
        j = wstate["next_use"]
        assert wlist[j][:3] == (kind, l, c0), (wlist[j], kind, l, c0)
        while wstate["next_load"] < min(j - hold + NSLOT, len(wlist)):
            w_emit_load(wstate["next_load"])
            wstate["next_load"] += 1
        wstate["next_use"] += 1
        slot = j % NSLOT
        return w_view(slot, kind, wlist[j][3]), bws[slot]

    P.op("pool", lambda e: e.memset(ident[:], 0.0), [], [bident])
    P.op("pool", lambda e: e.affine_select(out=ident[:], in_=ident[:], pattern=[[-1, 128]],
                                           compare_op=ALU.not_equal, fill=1.0, base=0,
                                           channel_multiplier=1), [bident], [bident])
    dve_memset(inv1024[:], 1.0 / 1024.0, [binv])
    dve_memset(inv512[:], 1.0 / 512.0, [binv])
    dve_memset(fence_cell[:], 0.0, [bfence])
    for g, wd in enumerate(WINDOWS):
        dve_memset(invc[:, g, :], 1.0 / wd, [binvc])
        for tcol in range(wd - 1):
            dve_memset(invc[:, g, tcol:tcol + 1], 1.0 / (tcol + 1), [binvc])
    dve_memset(halo[:], 0.0, [bhalo[0], bhalo[1]])

    s0, bs0 = STG.next()
    vrows = stg[0:64, s0, 0:128]
    dma("sp", stg[0:16, s0, 0:128], g_mix.rearrange("l (c p) -> (l c) p", p=128), [], [bs0], ("c", 0))
    dma("sp", stg[16:32, s0, 0:128], g_ffn.rearrange("l (c p) -> (l c) p", p=128), [], [bs0], ("c", 1))
    dma("sp", stg[32:48, s0, 0:128], g_branch.rearrange("l (c p) -> (l c) p", p=128), [], [bs0], ("c", 2))
    dma("sp", stg[48:56, s0, 0:128], g_final.rearrange("l (c p) -> (l c) p", p=128), [], [bs0], ("c", 3))
    dma("sp", stg[56:64, s0, 0:128], pool_scale.rearrange("l (c p) -> (l c) p", p=128), [], [bs0], ("c", 0))
    tr, btr = TR.next()
    tr_op(tr[:, 0:64], vrows, ident[0:64, 0:64], [bs0], [btr])
    dve_copy(vecT[:, :], tr[:, 0:64], [btr], [bvec])
    for l in range(2):
        dma("sp", gvb[:, l, :], g_v[l:l + 1, :].partition_broadcast(128), [], [bgvb], ("c", 1))
        for c in range(4):
            for half in range(2):
                hh = 2 * c + half
                dma("sp", biasb[half * 64:(half + 1) * 64, l, c, :],
                    b_sp[l, hh:hh + 1, :].partition_broadcast(64), [], [bbias], ("c", 2 + half))
    for l in range(2):
        s1, bs1 = STG.next()
        wsn = stg[:, s1, :].rearrange("p (h j) -> p h j", h=8)
        dma("sp", wsn, w_sp[l].rearrange("h i j -> i h j"), [], [bs1], ("c", 0))
        dve_memset(wsn[0:64, :, 64:128], 0.0, [bs1])
        for half in range(2):
            tr, btr = TR.next()
            trv = tr[:, :].rearrange("p (j f) -> p j f", j=4)
            for j in range(4):
                tr_op(trv[:, j, :], wsn[:, half * 4 + j, :], ident[:, :], [bs1], [btr])
            dve_copy(WsT[:, l, half * 4:half * 4 + 4, :], trv, [btr], [bwst])
        dma("pool", wpl[:, l, :, :], w_pool[l].rearrange("g c d -> c g d"), [], [bwpl], ("c", 4))
    if SAMPLE:
        for l in range(2):
            s1, bs1 = STG.next()
            dma("sp", stg[0:HIST, s1, 0:512], sst[l], [], [bs1], ("c", 1))
            tr, btr = TR.next()
            trv = tr[:, :].rearrange("p (j f) -> p j f", j=4)
            for c in range(4):
                tr_op(trv[:, c, 0:HIST], stg[0:HIST, s1, c * 128:(c + 1) * 128], ident[0:HIST, 0:HIST],
                      [bs1], [btr])
            dve_copy(halos[:, l, :, 0:HIST], trv[:, :, 0:HIST], [btr], [bhalos[l]])
    for b in (bident, binv, bvec, bgvb, bbias, bwst, bwpl, binvc):
        b.const = True

    pend = []

    def flush():
        for f in pend:
            f()
        del pend[:]

    def cols(g):
        return slice(g.c0, g.c0 + g.w)

    def stats_sq(g, src_ap, src_buf, st, bst, first, last, invt, defer=True):
        w = g.w
        s, bs = SQ.next()
        act_op(sqr[:, s, 0:w], src_ap, AF.Square, [src_buf], [bs])

        def f():
            mm_op(st[:, 0:w], invt[:, :], sqr[:, s, 0:w], first, last, [bs, binv], [bst])
        if defer:
            pend.append(f)
        else:
            f()

    def rstd_finish(g, st, bst):
        w = g.w
        r, br = RS.next()
        act_op(rsd[:, r, 0:w], st[:, 0:w], AF.Ln, [bst], [br], bias=eps_ap[:, 0:1])
        act_op(rsd[:, r, 0:w], rsd[:, r, 0:w], AF.Exp, [br], [br], scale=-0.5)
        AUX.release(bst)
        return r, br

    def norm_apply(g, src_ap, src_buf, n, r, br, gcol, dst_ap, dst_buf):
        w = g.w
        for c in range(n):
            dve_stt(dst_ap(c), src_ap(c), gcol(c), rsd[:, r, 0:w], ALU.mult, ALU.mult,
                    [src_buf(c), br, bvec], [dst_buf(c)])

    def prep_norm(g, st, gcol, dst_ap, dst_buf):
        flush()
        r, br = rstd_finish(g, *st)
        norm_apply(g, lambda c: xT[:, c, cols(g)], lambda c: bx[c][g.gi], KC, r, br, gcol, dst_ap, dst_buf)

    def prep_norm1(l):
        return lambda g, st: prep_norm(g, st, lambda c: gmix(l, c), lambda c: hT[:, c, cols(g)],
                                       lambda c: bh[c][g.gi])

    def prep_ffn(l):
        return lambda g, st: prep_norm(g, st, lambda c: gffn(l, c), lambda c: hT[:, c, cols(g)],
                                       lambda c: bh[c][g.gi])

    def prep_final():
        return lambda g, st: prep_norm(g, st, lambda c: gfin(c), lambda c: xT[:, c, cols(g)],
                                       lambda c: bx[c][g.gi])

    def full_stats(g):
        st, bst = AUX.next(hold=True)
        for c in range(KC):
            stats_sq(g, xT[:, c, cols(g)], bx[c][g.gi], st, bst, c == 0, c == KC - 1, inv1024, defer=False)
        return st, bst

    prefetched = {}

    def load_dma(g, b, hold=False):
        src = xp if g.kind == "p" else xs
        bw = g.bw
        r0 = g.row0 + b * 128
        s, bs = STG.next(hold=hold)
        dma("pool", stg[0:bw, s, :], src[r0:r0 + bw, :], [], [bs], ("ld", s))
        return s, bs

    def prefetch_loads(nxt, nblocks=4):
        if nxt is None:
            return
        n = 0
        for g in nxt.groups:
            for b in range(g.nblk):
                if n < nblocks:
                    prefetched[(id(nxt), g.gi, b)] = load_dma(g, b, hold=True)
                    n += 1

    def load_block(t, g, b):
        bw = g.bw
        key = (id(t), g.gi, b)
        if key in prefetched:
            s, bs = prefetched.pop(key)
            STG.release(bs)
        else:
            s, bs = load_dma(g, b)
        col = g.c0 + b * 128
        for half in range(2):
            tr, btr = AUX.next()
            trv = tr[:, :].rearrange("p (j f) -> p j f", j=4)
            for j in range(4):
                c = half * 4 + j
                tr_op(trv[:, j, 0:bw], stg[0:bw, s, c * 128:(c + 1) * 128], ident[0:bw, 0:bw],
                      [bs], [btr])
            dve_copy(xT[:, half * 4:half * 4 + 4, col:col + bw], trv[:, :, 0:bw],
                     [btr], [bx[half * 4 + j][g.gi] for j in range(4)])

    def load_tile(t):
        sts = {}
        for g in t.groups:
            for b in range(g.nblk):
                load_block(t, g, b)
        for g in t.groups:
            sts[g.gi] = full_stats(g)
        return sts

    def mixer(t, l, st_in, done):
        G = t.groups
        flush()
        for g in G:
            if g.gi not in done:
                prep_norm1(l)(g, st_in[g.gi])
        for g in G:
            hcols = slice(g.xb0 - HIST, g.xb0)
            if g.halo_src == "halos":
                dve_copy(xbT[:, :, hcols], halos[:, l, :, 0:HIST], [bhalos[l]], [bxbh2])
            elif g.halo_src == "zero":
                dve_memset(xbT[:, :, hcols], 0.0, [bxbh])
            elif g.halo_src == "halo":
                dve_copy(xbT[:, :, hcols], halo[:, l, :, 0:HIST], [bhalo[l]], [bxbh])
        wv, bwv = w_acquire("win", l, 512)
        wx, bwx = w_acquire("win", l, 1024, hold=1)
        for g in G:
            sv, bsv = SSV.next()
            bw = g.bw
            nb_ = g.nblk
            vts = []
            for b in range(nb_):
                col = g.c0 + b * 128
                mm, bmm = MM.next()
                for k in range(KC):
                    mm_op(mm[0:bw, :], hT[:, k, col:col + bw], wv[:, k, :], k == 0, k == KC - 1,
                          [bh[k][g.gi], bwv], [bmm])
                vt, bvt = VT.next()
                vts.append((vt, bvt))
                act_op(vtmp[0:bw, vt, :], mm[0:bw, :], AF.Gelu, [bmm], [bvt])
                s, bs = SQ.next()
                act_op(sqr[0:bw, s, :], vtmp[0:bw, vt, :], AF.Square, [bvt], [bs, bsv],
                       accum_out=ssv[0:bw, sv * 4 + b:sv * 4 + b + 1])
            P.op("dve", lambda e, sv=sv, bw=bw, nb_=nb_: e.tensor_scalar(
                out=rv[0:bw, sv * 4:sv * 4 + nb_], in0=ssv[0:bw, sv * 4:sv * 4 + nb_], scalar1=1.0 / DG,
                scalar2=EPS, op0=ALU.mult, op1=ALU.add), [bsv], [bsv])
            act_op(rv[0:bw, sv * 4:sv * 4 + nb_], rv[0:bw, sv * 4:sv * 4 + nb_], AF.Ln, [bsv], [bsv])
            act_op(rv2[0:bw, sv * 4:sv * 4 + nb_], rv[0:bw, sv * 4:sv * 4 + nb_], AF.Exp, [bsv], [bsv],
                   scale=-0.5)
            for b in range(nb_):
                vt, bvt = vts[b]
                bi = g.gi * 4 + b
                rcol = rv2[0:bw, sv * 4 + b:sv * 4 + b + 1]
                if g.kind == "s":
                    dve_stt(vout[0:bw, :], vtmp[0:bw, vt, :], rcol, gvb[0:bw, l, :],
                            ALU.mult, ALU.mult, [bvt, bsv, bgvb], [bvout])
                    dve_copy(vn[0:bw, bi, :], vout[0:bw, :], [bvout], [bvn[bi]])
                    dma("sp", vs[l, 0:bw, :], vout[0:bw, :], [bvout], [], ("vout", 0), is_out=True)
                else:
                    dve_stt(vn[0:bw, bi, :], vtmp[0:bw, vt, :], rcol, gvb[0:bw, l, :],
                            ALU.mult, ALU.mult, [bvt, bsv, bgvb], [bvn[bi]])
            for c in range(4):
                mm, bmm = MM.next()
                for k in range(KC):
                    mm_op(mm[:, 0:g.w], wx[:, k, c * 128:(c + 1) * 128], hT[:, k, cols(g)],
                          k == 0, k == KC - 1, [bh[k][g.gi], bwx], [bmm])
                act_op(xbT[:, c, g.xb0:g.xb0 + g.w], mm[:, 0:g.w], AF.Copy,
                       [bmm], [bxb[c][g.gi]])
        for gl in G:
            if gl.halo_save:
                dve_copy(halo[:, l, :, 0:HIST], xbT[:, :, gl.xb0 + gl.w - HIST:gl.xb0 + gl.w],
                         [bxb[c][gl.gi] for c in range(4)], [bhalo[l]])
        for gl in G:
            if gl.hist_out is None:
                continue
            tr, btr = AUX.next()
            trv = tr[:, :].rearrange("p (j f) -> p j f", j=4)
            for c in range(4):
                tr_op(trv[0:HIST, c, :], xbT[:, c, gl.xb0 + gl.w - HIST:gl.xb0 + gl.w], ident[:, :],
                      [bxb[c][gl.gi]], [btr])
            dve_copy(hst[0:HIST, :].rearrange("p (j f) -> p j f", j=4), trv[0:HIST, :, :], [btr], [bhst])
            dst = hp[l, gl.hist_out[1]] if gl.hist_out[0] == "p" else hs[l]
            dma("sp", dst, hst[0:HIST, :], [bhst], [], ("hst", 0), is_out=True)
        def pool_dve(g, gi):
            w = g.w
            wd = WINDOWS[gi]
            if True:
                base = g.xb0

                def X(a, b):
                    return xbT[:, gi, base + a:base + b]

                prevb = {"prev": bxb[gi][g.gi - 1], "halos": bxbh2}.get(g.halo_src, bxbh)
                rd = [bxb[gi][g.gi], prevb]
                ext = wd - 2
                pool_tt(ptmp[:, 0, 0:w + ext], X(-ext, w), X(-ext - 1, w - 1), ALU.add, rd, [bpt[0]])
                cur = 0
                span = 2
                while span < wd:
                    ext2 = wd - 2 * span
                    nxt = (cur + 1) % 3
                    pool_tt(ptmp[:, nxt, 0:w + ext2], ptmp[:, cur, span:span + w + ext2],
                           ptmp[:, cur, 0:w + ext2], ALU.add, [bpt[cur]], [bpt[nxt]])
                    cur = nxt
                    span *= 2
                nx2 = (cur + 1) % 3
                pool_tt(ptmp[:, nx2, 0:w], ptmp[:, cur, 0:w], invc[:, gi, 15:16].to_broadcast([128, w]),
                        ALU.mult, [bpt[cur], binvc], [bpt[nx2]])
                pool_tt(dT[:, gi, cols(g)], ptmp[:, nx2, 0:w], X(0, w), ALU.subtract,
                        [bpt[nx2], bxb[gi][g.gi]], [bd[gi][g.gi]])
                if g.first:
                    pool_tt(tiny[:, 0:16], ptmp[:, cur, 0:16], invc[:, gi, :], ALU.mult,
                           [bpt[cur], binvc], [btiny])
                    pool_tt(dT[:, gi, g.c0:g.c0 + 16], tiny[:, 0:16], X(0, 16), ALU.subtract,
                           [btiny, bxb[gi][g.gi]], [bd[gi][g.gi]])
        for g in G:
            for gi in range(4):
                pool_dve(g, gi)
        wu, bwu = w_acquire("win", l, 0)
        sts_a = {}
        sts_b = {}
        pool_q = []

        def pool_mm(g, gi, stb, bstb):
            w = g.w
            mm, bmm = MM.next()
            mm_op(mm[:, 0:w], wpl[:, l, gi, :], dT[:, gi, cols(g)], True, True,
                  [bd[gi][g.gi], bwpl], [bmm])
            flush()
            act_op(ybT[:, gi, cols(g)], mm[:, 0:w], AF.Copy, [bmm, bvec], [byb[gi][g.gi]],
                   scale=psc(l, gi))
            if not g.small:
                stats_sq(g, ybT[:, gi, cols(g)], byb[gi][g.gi], stb, bstb, gi == 0, gi == 3, inv512)

        def pool_drain(g=None, n=None):
            k = 0
            while pool_q and (g is None or pool_q[0][0] is g) and (n is None or k < n):
                pool_q.pop(0)[1]()
                k += 1

        wprep = {}

        def prep_wout(g, part=None):
            if part in (None, 0):
                pool_drain(g=g)
                flush()
                if g.small:
                    sta, bsta = AUX.next(hold=True)
                    for c in range(4):
                        stats_sq(g, yaT[:, c, cols(g)], bya[c][g.gi], sta, bsta, c == 0, c == 3, inv512,
                                 defer=False)
                    ra, bra = rstd_finish(g, sta, bsta)
                    stb, bstb = AUX.next(hold=True)
                    for c in range(4):
                        stats_sq(g, ybT[:, c, cols(g)], byb[c][g.gi], stb, bstb, c == 0, c == 3, inv512,
                                 defer=False)
                    rb, brb = rstd_finish(g, stb, bstb)
                else:
                    ra, bra = rstd_finish(g, *sts_a[g.gi])
                    rb, brb = rstd_finish(g, *sts_b[g.gi])
                norm_apply(g, lambda c: yaT[:, c, cols(g)], lambda c: bya[c][g.gi], 4, ra, bra,
                           lambda c: gbr(l, c), lambda c: hT[:, c, cols(g)], lambda c: bh[c][g.gi])
                wprep[g.gi] = (rb, brb)
            if part in (None, 1):
                rb, brb = wprep.pop(g.gi)
                norm_apply(g, lambda c: ybT[:, c, cols(g)], lambda c: byb[c][g.gi], 4, rb, brb,
                           lambda c: gbr(l, 4 + c), lambda c: hT[:, 4 + c, cols(g)], lambda c: bh[4 + c][g.gi])
                wdone.add(g.gi)

        wdone = set()
        for kspu, g in enumerate(G):
            w = g.w
            bw = g.bw
            if g.small:
                sta = bsta = stb = bstb = None
            else:
                sta, bsta = AUX.next(hold=True)
                stb, bstb = AUX.next(hold=True)
                sts_a[g.gi] = (sta, bsta)
                sts_b[g.gi] = (stb, bstb)
            for c in range(4):
                mmx, bmx = MM.next()
                for half in range(2):
                    hh = 2 * c + half
                    for b in range(g.nblk):
                        bi = g.gi * 4 + b
                        mm_op(mmx[half * 64:(half + 1) * 64, b * 128:b * 128 + bw],
                              vn[0:bw, bi, hh * 64:(hh + 1) * 64], WsT[0:bw, l, hh, 0:bw], True, True,
                              [bvn[bi], bwst], [bmx])
                mmy, bmy = MM.next()
                for k in range(KC):
                    mm_op(mmy[:, 0:w], wu[:, k, c * 128:(c + 1) * 128], hT[:, k, cols(g)],
                          k == 0, k == KC - 1, [bh[k][g.gi], bwu], [bmy])
                flush()
                pool_drain(n=2)
                if kspu >= 1 and c == 2:
                    prep_wout(G[kspu - 1], 0)
                if kspu >= 1 and c == 3:
                    prep_wout(G[kspu - 1], 1)
                ut, but = UT.next()
                act_op(utmp[:, ut, 0:w], mmy[:, 0:w], AF.Gelu, [bmy], [but])
                tt, btt = TTR.next()
                if g.nblk > 1:
                    dve_tt(ttmp[:, tt, 0:w].rearrange("p (b i) -> p b i", b=g.nblk),
                           mmx[:, 0:w].rearrange("p (b i) -> p b i", b=g.nblk),
                           biasb[:, l, c, :].unsqueeze(1).to_broadcast([128, g.nblk, 128]),
                           ALU.add, [bmx, bbias], [btt])
                else:
                    dve_tt(ttmp[:, tt, 0:w], mmx[:, 0:w], biasb[:, l, c, 0:w], ALU.add, [bmx, bbias], [btt])
                dve_tt(yaT[:, c, cols(g)], ttmp[:, tt, 0:w], utmp[:, ut, 0:w], ALU.mult,
                       [btt, but], [bya[c][g.gi]])
                if not g.small:
                    stats_sq(g, yaT[:, c, cols(g)], bya[c][g.gi], sta, bsta, c == 0, c == 3, inv512)
            for gi in range(4):
                pool_q.append((g, lambda g=g, gi=gi, stb=stb, bstb=bstb: pool_mm(g, gi, stb, bstb)))
        wo = [None, None]
        bwo = [None, None]
        wo[0], bwo[0] = w_acquire("wout", l, 0)
        wo[1], bwo[1] = w_acquire("wout", l, 512, hold=1)
        st_out = {}
        done_ffn = set()

        if G[0].gi not in wdone:
            prep_wout(G[0])
        for k, g in enumerate(G):
            st, bst = AUX.next(hold=True)
            st_out[g.gi] = (st, bst)
            u = 0
            for ch in range(2):
                for oc in range(4):
                    o = ch * 4 + oc
                    mm, bmm = MM.next()
                    for kk in range(KC):
                        mm_op(mm[:, 0:g.w], wo[ch][:, kk, oc * 128:(oc + 1) * 128], hT[:, kk, cols(g)],
                              kk == 0, kk == KC - 1, [bh[kk][g.gi], bwo[ch]], [bmm])
                    flush()
                    pool_drain(n=1)
                    dve_tt(xT[:, o, cols(g)], xT[:, o, cols(g)], mm[:, 0:g.w], ALU.add,
                           [bx[o][g.gi], bmm], [bx[o][g.gi]])
                    stats_sq(g, xT[:, o, cols(g)], bx[o][g.gi], st, bst, o == 0, o == 7, inv1024)
                    u += 1
                    if u == 2 and k >= 1:
                        pg = G[k - 1]
                        prep_ffn(l)(pg, st_out[pg.gi])
                        done_ffn.add(pg.gi)
                    if u == 5 and k + 1 < len(G) and G[k + 1].gi not in wdone:
                        prep_wout(G[k + 1])
        return st_out, done_ffn

    def ffn(t, l, st_in, done, next_prep, nxt_tile=None):
        G = t.groups
        for g in G:
            if g.gi not in done:
                prep_ffn(l)(g, st_in[g.gi])
        for j in range(6):
            nco = 512 if j < 5 else 256
            wg, bwg = w_acquire("gate", l, j * 512)
            wu, bwu = w_acquire("up", l, j * 512, hold=1)
            for g in G:
                w = g.w
                for fc in range(nco // 128):
                    f = j * 4 + fc
                    ma, bma = MM.next()
                    for k in range(KC):
                        mm_op(ma[:, 0:w], wg[:, k, fc * 128:(fc + 1) * 128], hT[:, k, cols(g)],
                              k == 0, k == KC - 1, [bh[k][g.gi], bwg], [bma])
                    mb, bmb = MM.next()
                    for k in range(KC):
                        mm_op(mb[:, 0:w], wu[:, k, fc * 128:(fc + 1) * 128], hT[:, k, cols(g)],
                              k == 0, k == KC - 1, [bh[k][g.gi], bwu], [bmb])
                    sm, bsm = STM.next()
                    act_op(stmp[:, sm, 0:w], ma[:, 0:w], AF.Silu, [bma], [bsm])
                    dve_tt(actT[:, f, cols(g)], stmp[:, sm, 0:w], mb[:, 0:w], ALU.mult,
                           [bsm, bmb], [bact[f][g.gi]])
        st_out = {}
        done_next = set()
        prefetch_loads(nxt_tile)
        for g in G:
            st_out[g.gi] = AUX.next(hold=True)
        for op_ in range(4):
            wd_ = [None, None]
            bwd = [None, None]
            wd_[0], bwd[0] = w_acquire("down", l, (2 * op_) * 128)
            wd_[1], bwd[1] = w_acquire("down", l, (2 * op_ + 1) * 128, hold=1)
            for kg, g in enumerate(G):
                st, bst = st_out[g.gi]
                for i2 in range(2):
                    o = 2 * op_ + i2
                    mm, bmm = MM.next()
                    for k in range(FC):
                        mm_op(mm[:, 0:g.w], wd_[i2][:, k, :], actT[:, k, cols(g)], k == 0, k == FC - 1,
                              [bact[k][g.gi], bwd[i2]], [bmm])
                    flush()
                    dve_tt(xT[:, o, cols(g)], xT[:, o, cols(g)], mm[:, 0:g.w], ALU.add,
                           [bx[o][g.gi], bmm], [bx[o][g.gi]])
                    stats_sq(g, xT[:, o, cols(g)], bx[o][g.gi], st, bst, o == 0, o == 7, inv1024)
                    if op_ == 3 and i2 == 1 and kg >= 1:
                        pg = G[kg - 1]
                        next_prep(pg, st_out[pg.gi])
                        done_next.add(pg.gi)
        return st_out, done_next

    def final_block(t, g, b):
        dst = yp if g.kind == "p" else ys
        bw = g.bw
        col = g.c0 + b * 128
        r0 = g.row0 + b * 128
        s, bs = STG.next()
        for half in range(2):
            tr, btr = AUX.next()
            trv = tr[:, :].rearrange("p (j f) -> p j f", j=4)
            for j in range(4):
                c = half * 4 + j
                tr_op(trv[0:bw, j, :], xT[:, c, col:col + bw], ident[:, :], [bx[c][g.gi]], [btr])
            act_op(stg[0:bw, s, half * 512:(half + 1) * 512].rearrange("p (j f) -> p j f", j=4),
                   trv[0:bw, :, :], AF.Copy, [btr], [bs])
        dma("sp", dst[r0:r0 + bw, :], stg[0:bw, s, :], [bs], [], ("stg", s), is_out=True)

    def final_and_load(t, st_in, nxt, done):
        for g in t.groups:
            if g.gi not in done:
                prep_final()(g, st_in[g.gi])
        ngroups = {g.gi: g for g in (nxt.groups if nxt is not None else [])}
        pendL = []
        for g in t.groups:
            for b in range(g.nblk):
                final_block(t, g, b)
                if pendL:
                    ng, nb = pendL.pop(0)
                    load_block(nxt, ng, nb)
            if g.gi in ngroups:
                ng = ngroups.pop(g.gi)
                pendL += [(ng, nb) for nb in range(ng.nblk)]
        for ng, nb in pendL:
            load_block(nxt, ng, nb)
        for ng in ngroups.values():
            for nb in range(ng.nblk):
                load_block(nxt, ng, nb)
        sts = {}
        if nxt is not None:
            for g in nxt.groups:
                sts[g.gi] = full_stats(g)
        return sts

    eps_ap = sb("eps_ap", [128, 1], F32)
    beps = Buf("eps")
    dve_memset(eps_ap[:], EPS, [beps])
    beps.const = True
    P.op("act", lambda e: e.activation(out=fence_cell[:, 1:2], in_=eps_ap[:, 0:1], func=AF.Copy),
         [beps], [bfence])

    st = load_tile(tiles[0])
    for ti, t in enumerate(tiles):
        done = set()
        for l in range(NLAYER):
            fence()
            st, done = mixer(t, l, st, done)
            if DEBUG and t is tiles[0] and l == 0:
                dma("sp", dbg_x1, xT[:, :, :], [b2 for row in bx for b2 in row], [], ("dbg", 4), is_out=True)
            fence()
            nprep = prep_norm1(l + 1) if l + 1 < NLAYER else prep_final()
            nxt_t = tiles[ti + 1] if (l + 1 == NLAYER and ti + 1 < len(tiles)) else None
            st, done = ffn(t, l, st, done, nprep, nxt_t)
        fence()
        st = final_and_load(t, st, tiles[ti + 1] if ti + 1 < len(tiles) else None, done)
    flush()

    assert wstate["next_use"] == len(wlist)
    P.finalize(nc)
    return nc, P


_CACHE = {}


def kernel(x_prompt, x_sample, state_pool, g_mix, w_in, g_v, w_spatial, b_spatial, w_pool, pool_scale,
           g_branch, w_out, g_ffn, w_gate, w_up, w_down, g_final):
    f = lambda a: np.ascontiguousarray(np.asarray(a, dtype=np.float32))
    x_prompt = f(x_prompt)
    x_sample = f(x_sample)
    state_pool = f(state_pool)
    B, S, _ = x_prompt.shape
    nseq = B // N_CORES
    if "nc" not in _CACHE:
        _CACHE["nc"] = build_program(NSEQ=nseq, SEQ=S, SAMPLE=True)[0]
    nc = _CACHE["nc"]
    shared = {
        "g_mix": f(g_mix), "w_in": f(w_in), "g_v": f(g_v), "w_sp": f(w_spatial), "b_sp": f(b_spatial),
        "w_pool": f(w_pool), "pool_scale": f(pool_scale), "g_branch": f(g_branch), "w_out": f(w_out),
        "g_ffn": f(g_ffn), "w_gate": f(w_gate), "w_up": f(w_up), "w_down": f(w_down),
        "g_final": f(g_final).reshape(1, D),
    }
    in_maps = []
    for i in range(N_CORES):
        m = dict(shared)
        m["xp"] = x_prompt[i * nseq:(i + 1) * nseq].reshape(nseq * S, D)
        m["xs"] = x_sample[i]
        m["sst"] = np.ascontiguousarray(state_pool[:, i])
        in_maps.append(m)
    res = run_bass_kernel_spmd(nc, in_maps, core_ids=list(range(N_CORES)))
    rs = res.results
    y_prompt = np.concatenate([r["yp"].reshape(nseq, S, D) for r in rs], axis=0)
    y_sample = np.stack([r["ys"] for r in rs], axis=0)
    hp = np.concatenate([r["hp"] for r in rs], axis=1)
    hs = np.stack([r["hs"] for r in rs], axis=1)
    vs = np.stack([r["vs"] for r in rs], axis=1)
    return (y_prompt.astype(np.float32), y_sample.astype(np.float32), hp.astype(np.float32),
            hs.astype(np.float32), vs.astype(np.float32))
```
